# Optimizing a Trainium2 kernel written in Bass

```python
import jax, jax.numpy as jnp
from jax import lax
import numpy as np

D_MODEL = 1024
BATCH = 4
SEQ = 8192
DEPTH = 2

N_HEADS = 16
N_KV_HEADS = 4
HEAD_DIM = 64
GROUP = N_HEADS // N_KV_HEADS
Q_BLOCK = 128
ROPE_THETA = 10000.0
ROPE_HALF = HEAD_DIM // 2
GRID_W = 64
D_RNN = 1024
N_RNN_BLOCKS = 16
RNN_BLOCK = D_RNN // N_RNN_BLOCKS
CONV_W = 4
CONV_PAD = (2, 1)
LRU_C = 8.0
D_FF = 2816
FFN_RESID = 0.5
N_BRANCH = 2
EPS = 1e-6
Q_COLS = N_HEADS * HEAD_DIM
KV_COLS = N_KV_HEADS * HEAD_DIM
IN_COLS = Q_COLS + 2 * KV_COLS + 2 * D_RNN + N_BRANCH * D_MODEL
SPLITS = (Q_COLS, Q_COLS + KV_COLS, Q_COLS + 2 * KV_COLS,
          Q_COLS + 2 * KV_COLS + D_RNN, Q_COLS + 2 * KV_COLS + 2 * D_RNN)

kernel_name = "griffin_gated_gqa_rglru_macaron_encoder"


def rmsnorm(x, g):
    xf = x.astype(jnp.float32)
    y = xf * lax.rsqrt(jnp.mean(xf * xf, axis=-1, keepdims=True) + EPS)
    return (y * g.astype(jnp.float32)).astype(x.dtype)


def swiglu(x, w1, w2):
    gu = x @ w1
    g, u = jnp.split(gu, 2, axis=-1)
    return (jax.nn.silu(g) * u) @ w2


def rope_tables(seq):
    rows = seq // GRID_W
    row = jnp.repeat(jnp.arange(rows, dtype=jnp.int32), GRID_W).astype(jnp.float32)
    col = jnp.tile(jnp.arange(GRID_W, dtype=jnp.int32), rows).astype(jnp.float32)
    inv = ROPE_THETA ** (-jnp.arange(0, ROPE_HALF, 2, dtype=jnp.float32) / ROPE_HALF)
    ang_r = row[:, None] * inv[None, :]
    ang_c = col[:, None] * inv[None, :]
    return (jnp.cos(ang_r)[:, None], jnp.sin(ang_r)[:, None],
            jnp.cos(ang_c)[:, None], jnp.sin(ang_c)[:, None])


def apply_rope(x, tabs):
    cr, sr, cc, sc = tabs
    xf = x.astype(jnp.float32)
    r1, r2, c1, c2 = jnp.split(xf, 4, axis=-1)
    out = jnp.concatenate([r1 * cr - r2 * sr, r2 * cr + r1 * sr,
                           c1 * cc - c2 * sc, c2 * cc + c1 * sc], axis=-1)
    return out.astype(x.dtype)


def block_attention(q, k, v):
    b, s = q.shape[0], q.shape[1]
    nb = s // Q_BLOCK
    scale = HEAD_DIM ** -0.5
    qb = (q * scale).reshape(b, nb, Q_BLOCK, N_KV_HEADS, GROUP, HEAD_DIM).transpose(1, 0, 2, 3, 4, 5)

    def one_block(qi):
        sc = jnp.einsum('bqhgd,bkhd->bhgqk', qi, k).astype(jnp.float32)
        p = jax.nn.softmax(sc, axis=-1)
        return jnp.einsum('bhgqk,bkhd->bqhgd', p.astype(v.dtype), v)

    o = lax.map(one_block, qb)
    return o.transpose(1, 0, 2, 3, 4, 5).reshape(b, s, Q_COLS)


def depthwise_conv(x, w, bias):
    y = lax.conv_general_dilated(x, w[:, None, :], window_strides=(1,), padding=[CONV_PAD],
                                 dimension_numbers=('NWC', 'WIO', 'NWC'),
                                 feature_group_count=D_RNN)
    return y + bias


def _lin_combine(c1, c2):
    a1, b1 = c1
    a2, b2 = c2
    return a1 * a2, a2 * b1 + b2


def rg_lru(x, wa, ba, wx, bx, lam, reverse):
    b, s, _ = x.shape
    xb = x.reshape(b, s, N_RNN_BLOCKS, RNN_BLOCK)
    r = jax.nn.sigmoid((jnp.einsum('bsnc,ncd->bsnd', xb, wa).reshape(b, s, D_RNN) + ba).astype(jnp.float32))
    i = jax.nn.sigmoid((jnp.einsum('bsnc,ncd->bsnd', xb, wx).reshape(b, s, D_RNN) + bx).astype(jnp.float32))
    log_a = -LRU_C * r * jax.nn.softplus(-lam.astype(jnp.float32))
    a = jnp.exp(log_a)
    u = jnp.sqrt(-jnp.expm1(2.0 * log_a)) * (i * x.astype(jnp.float32))
    _, h = lax.associative_scan(_lin_combine, (a, u), axis=1, reverse=reverse)
    return h


def setup_inputs(seed: int = 0) -> dict:
    key = jax.random.key(seed)
    ks = jax.random.split(key, 24)
    f32 = jnp.float32

    def nrm(k, shape, scale):
        return jax.random.normal(k, shape, f32) * scale

    def gain(k, shape):
        return 1.0 + 0.02 * jax.random.normal(k, shape, f32)

    L = DEPTH
    u = jax.random.uniform(ks[16], (L, 2, D_RNN), f32, 0.9, 0.999)
    p = u ** (1.0 / LRU_C)
    lam = jnp.log(p) - jnp.log1p(-p)
    return {
        "x": jax.random.normal(ks[0], (BATCH, SEQ, D_MODEL), f32),
        "ffn1_norm": gain(ks[1], (L, D_MODEL)),
        "ffn1_w1": nrm(ks[2], (L, D_MODEL, 2 * D_FF), D_MODEL ** -0.5),
        "ffn1_w2": nrm(ks[3], (L, D_FF, D_MODEL), D_FF ** -0.5),
        "mix_norm": gain(ks[4], (L, D_MODEL)),
        "w_in": nrm(ks[5], (L, D_MODEL, IN_COLS), D_MODEL ** -0.5),
        "b_gate": nrm(ks[6], (L, N_BRANCH, D_MODEL), 0.01),
        "q_norm": gain(ks[7], (L, HEAD_DIM)),
        "k_norm": gain(ks[8], (L, HEAD_DIM)),
        "w_attn_o": nrm(ks[9], (L, Q_COLS, D_MODEL), Q_COLS ** -0.5),
        "conv_w": nrm(ks[10], (L, CONV_W, D_RNN), CONV_W ** -0.5),
        "conv_b": nrm(ks[11], (L, D_RNN), 0.01),
        "lru_wa": nrm(ks[12], (L, 2, N_RNN_BLOCKS, RNN_BLOCK, RNN_BLOCK), RNN_BLOCK ** -0.5),
        "lru_ba": nrm(ks[13], (L, 2, D_RNN), 0.01),
        "lru_wx": nrm(ks[14], (L, 2, N_RNN_BLOCKS, RNN_BLOCK, RNN_BLOCK), RNN_BLOCK ** -0.5),
        "lru_bx": nrm(ks[15], (L, 2, D_RNN), 0.01),
        "lru_lambda": lam,
        "w_rnn_o": nrm(ks[17], (L, D_RNN, D_MODEL), D_RNN ** -0.5),
        "w_out": nrm(ks[18], (L, D_MODEL, D_MODEL), D_MODEL ** -0.5),
        "ffn2_norm": gain(ks[19], (L, D_MODEL)),
        "ffn2_w1": nrm(ks[20], (L, D_MODEL, 2 * D_FF), D_MODEL ** -0.5),
        "ffn2_w2": nrm(ks[21], (L, D_FF, D_MODEL), D_FF ** -0.5),
    }


def reference(x, ffn1_norm, ffn1_w1, ffn1_w2, mix_norm, w_in, b_gate, q_norm, k_norm,
              w_attn_o, conv_w, conv_b, lru_wa, lru_ba, lru_wx, lru_bx, lru_lambda,
              w_rnn_o, w_out, ffn2_norm, ffn2_w1, ffn2_w2):
    b, s, _ = x.shape
    tabs = rope_tables(s)
    for l in range(DEPTH):
        x = x + FFN_RESID * swiglu(rmsnorm(x, ffn1_norm[l]), ffn1_w1[l], ffn1_w2[l])

        h = rmsnorm(x, mix_norm[l])
        proj = h @ w_in[l]
        q, k, v, xr, yr, gl = jnp.split(proj, SPLITS, axis=-1)

        q = apply_rope(rmsnorm(q.reshape(b, s, N_HEADS, HEAD_DIM), q_norm[l]), tabs)
        k = apply_rope(rmsnorm(k.reshape(b, s, N_KV_HEADS, HEAD_DIM), k_norm[l]), tabs)
        v = v.reshape(b, s, N_KV_HEADS, HEAD_DIM)
        attn = block_attention(q, k, v) @ w_attn_o[l]

        xc = depthwise_conv(xr, conv_w[l], conv_b[l])
        hr = (rg_lru(xc, lru_wa[l, 0], lru_ba[l, 0], lru_wx[l, 0], lru_bx[l, 0], lru_lambda[l, 0], False)
              + rg_lru(xc, lru_wa[l, 1], lru_ba[l, 1], lru_wx[l, 1], lru_bx[l, 1], lru_lambda[l, 1], True))
        rnn = (hr.astype(x.dtype) * jax.nn.gelu(yr)) @ w_rnn_o[l]

        gates = jax.nn.sigmoid(gl.reshape(b, s, N_BRANCH, D_MODEL) + b_gate[l])
        merged = gates[:, :, 0] * attn + gates[:, :, 1] * rnn
        x = x + merged @ w_out[l]

        x = x + FFN_RESID * swiglu(rmsnorm(x, ffn2_norm[l]), ffn2_w1[l], ffn2_w2[l])
    return x
```

```python
import numpy as np
import ml_dtypes
from contextlib import ExitStack
import concourse.bass as bass
import concourse.mybir as mybir
import concourse.bass_utils as _bu

F32 = mybir.dt.float32
BF16 = mybir.dt.bfloat16
AF = mybir.ActivationFunctionType
ALU = mybir.AluOpType

D = 1024
DFF = 2816
NFC = DFF // 128
BATCH = 4
SEQ = 8192
NH = 16
NKV = 4
HD = 64
INC = 5632
EPS = 1e-6
GRID_W = 64
ROPE_THETA = 10000.0


class Buf:
    __slots__ = ("name", "last_w", "readers", "sem", "dcount", "par")

    def __init__(self, name, par=False):
        self.name = name
        self.last_w = {}
        self.readers = {}
        self.sem = None
        self.dcount = 0
        self.par = par


class Sched:
    ENG = ("pe", "dve", "act", "pool", "sp")

    def __init__(self, nc, es, tag=""):
        self.nc = nc
        self.es = es
        self.tag = tag
        self.ops = {e: [] for e in self.ENG}
        self.cnt = {e: 0 for e in self.ENG}
        self.sem = {e: nc.alloc_semaphore(name=tag + "c_" + e) for e in self.ENG}
        self.seen = {e: {} for e in self.ENG}
        self.nsem = 5
        self.nbuf = 0

    def buf(self, name=None, par=False):
        self.nbuf += 1
        return Buf(name or ("b%d" % self.nbuf), par)

    def _wait(self, eng, deps):
        seen = self.seen[eng]
        for key, n in deps.items():
            if key[0] == "c" and key[1] == eng and eng == "pe":
                continue
            if seen.get(key, 0) >= n:
                continue
            seen[key] = n
            if key[0] == "c":
                sem, val = self.sem[key[1]], n
            else:
                sem, val = key[1], 16 * n
            self.ops[eng].append(lambda e, sem=sem, val=val: e.wait_ge(sem, val))

    @staticmethod
    def _merge(dst, src):
        for k, n in src.items():
            if dst.get(k, 0) < n:
                dst[k] = n

    def _deps(self, reads, writes):
        deps = {}
        for b in reads:
            self._merge(deps, b.last_w)
        for b in writes:
            self._merge(deps, b.last_w)
            self._merge(deps, b.readers)
        return deps

    def _commit(self, tok, reads, writes):
        key, n = tok
        for b in writes:
            if b.par:
                if b.last_w.get(key, 0) < n:
                    b.last_w[key] = n
            else:
                b.last_w = {key: n}
                b.readers = {}
        for b in reads:
            if b.readers.get(key, 0) < n:
                b.readers[key] = n

    def op(self, eng, fn, reads=(), writes=(), inc=True):
        self._wait(eng, self._deps(reads, writes))
        if inc:
            self.cnt[eng] += 1
            sem = self.sem[eng]
            self.ops[eng].append(lambda e, fn=fn, sem=sem: fn(e).then_inc(sem, 1))
            tok = (("c", eng), self.cnt[eng])
        else:
            self.ops[eng].append(lambda e, fn=fn: fn(e))
            tok = (("c", eng), self.cnt[eng] + 1)
        self._commit(tok, reads, writes)

    def dma(self, eng, out, in_, sb, reads=(), writes=()):
        self._wait(eng, self._deps(reads, writes))
        if sb.sem is None:
            sb.sem = {}
            sb.dcount = {}
        if eng not in sb.sem:
            sb.sem[eng] = self.nc.alloc_semaphore(name="%sd%s_%s" % (self.tag, eng, sb.name))
            sb.dcount[eng] = 0
            self.nsem += 1
        sb.dcount[eng] += 1
        sem = sb.sem[eng]
        cnt_ = sb.dcount[eng]
        self.ops[eng].append(lambda e, out=out, in_=in_, sem=sem: e.dma_start(out=out, in_=in_).then_inc(sem, 16))
        tok = (("d", sem), cnt_)
        self._commit(tok, reads, writes)

    def finish(self, bufs, eng="sp"):
        deps = {}
        for b in bufs:
            self._merge(deps, b.last_w)
            self._merge(deps, b.readers)
        self._wait(eng, deps)

    def run(self):
        ops = self.ops
        with self.nc.Block() as block:
            @block.tensor
            def _(e):
                for f in ops["pe"]:
                    f(e)

            @block.vector
            def _(e):
                for f in ops["dve"]:
                    f(e)

            @block.scalar
            def _(e):
                for f in ops["act"]:
                    f(e)

            @block.gpsimd
            def _(e):
                for f in ops["pool"]:
                    f(e)

            @block.sync
            def _(e):
                for f in ops["sp"]:
                    f(e)


_PH = [0]


class Ctx:
    def __init__(self, nc):
        self.nc = nc
        self.cleanup = nc.cleanup_on_exit()
        self.cleanup.__enter__()
        self.es = ExitStack()
        self.tag = "p%d" % _PH[0]
        _PH[0] += 1
        self.S = Sched(nc, self.es, self.tag)
        self.psum = self.es.enter_context(nc.psum_tensor(self.tag + "ps", [128, 8, 512], F32))
        self.n = 0

    def sb(self, shape, dtype, name=None):
        self.n += 1
        return self.es.enter_context(self.nc.sbuf_tensor(self.tag + "sb_" + (name or ("t%d" % self.n)), shape, dtype))

    def close(self):
        self.es.close()
        self.cleanup.__exit__(None, None, None)


def feat(ap):
    return ap.rearrange("(c p) t -> p c t", p=128)


def mm_group(S, out, pairs, reads, wbuf):
    n = len(pairs)
    for i, (l, r) in enumerate(pairs):
        S.op("pe", lambda e, l=l, r=r, i=i: e.matmul(out, lhsT=l, rhs=r, start=(i == 0), stop=(i == n - 1)),
             reads=reads, writes=[wbuf], inc=(i == n - 1))


def emit_rmsnorm(C, x, xb, h, hb, gs, gsb, T, ones, sq, sqb, ssb, ps_ss, lnv, lnb, rstd, rsb, nfeat=1024):
    S = C.S
    S.op("act", lambda e: e.activation(out=sq[:], in_=x[:], func=AF.Square), reads=[xb], writes=[sqb])
    mm_group(S, ps_ss, [(ones[:], sq[:, c, :]) for c in range(8)], [sqb, C.onesb], ssb)
    S.op("act", lambda e: e.activation(out=lnv[:], in_=ps_ss, func=AF.Ln, scale=1.0 / nfeat, bias=C.eps[:]),
         reads=[ssb, C.epsb], writes=[lnb])
    S.op("act", lambda e: e.activation(out=rstd[:], in_=lnv[:], func=AF.Exp, scale=-0.5), reads=[lnb], writes=[rsb])
    for c in range(8):
        S.op("dve", lambda e, c=c: e.scalar_tensor_tensor(out=h[:, c, :], in0=x[:, c, :], scalar=gs[:, c:c + 1],
                                                          in1=rstd[:], op0=ALU.mult, op1=ALU.mult),
             reads=[xb, rsb, gsb], writes=[hb])


def load_weight(C, wsb, w_d, nk, ncols, name, split=2):
    S = C.S
    wv = w_d.rearrange("(k p) n -> p k n", p=128)
    bufs = []
    step = (ncols + split - 1) // split
    for k in range(nk):
        pieces = []
        for s0 in range(0, ncols, step):
            b = S.buf("%s%d_%d" % (name, k, s0))
            s1 = min(ncols, s0 + step)
            S.dma("pool", wsb[:, k, s0:s1], wv[:, k, s0:s1], b, writes=[b])
            pieces.append(b)
        bufs.append(pieces)
    return bufs


def const_setup(C):
    S = C.S
    C.ones = C.sb([128, 128], BF16, "ones")
    C.onesb = S.buf("ones")
    C.eps = C.sb([128, 1], F32, "eps")
    C.epsb = S.buf("eps")
    S.op("pool", lambda e: e.memset(C.ones[:], 1.0), writes=[C.onesb])
    S.op("pool", lambda e: e.memset(C.eps[:], EPS), writes=[C.epsb])


def phase_ffn(nc, x_d, out_d, w1_d, w2_d, g_d, ntok):
    T = 256
    NT = ntok // T
    C = Ctx(nc)
    S = C.S
    const_setup(C)
    w1s = C.sb([128, 8, 2 * DFF], BF16, "w1s")
    w2s = C.sb([128, NFC, D], BF16, "w2s")
    gs = C.sb([128, 8], F32, "gs")
    gsb = S.buf("gs")
    S.dma("sp", gs[:], g_d, gsb, writes=[gsb])
    w1b = load_weight(C, w1s, w1_d, 8, 2 * DFF, "w1", split=2)
    w2b = load_weight(C, w2s, w2_d, NFC, D, "w2", split=1)
    w1all = [b for p in w1b for b in p]
    xs = [C.sb([128, 8, T], F32, "x%d" % i) for i in range(2)]
    xsb = [S.buf("x%d" % i) for i in range(2)]
    sq = C.sb([128, 8, T], BF16, "sq"); sqb = S.buf("sq")
    h = C.sb([128, 8, T], BF16, "h"); hb = S.buf("h")
    lnv = C.sb([128, T], F32, "lnv"); lnb = S.buf("lnv")
    rstd = C.sb([128, T], F32, "rstd"); rsb = S.buf("rstd")
    hm = C.sb([128, NFC, T], BF16, "hm"); hmb = S.buf("hm")
    sg = [C.sb([128, T], F32, "sg%d" % i) for i in range(2)]
    sgb = [S.buf("sg%d" % i) for i in range(2)]
    ps = C.psum
    psb = [S.buf("ps%d" % i) for i in range(8)]
    xin = feat(x_d)
    xout = feat(out_d)
    outb = S.buf("out", par=True)

    def load(i):
        S.dma("sp", xs[i % 2][:], xin[:, :, i * T:(i + 1) * T], xsb[i % 2], writes=[xsb[i % 2]])

    load(0)
    for i in range(NT):
        x = xs[i % 2]; xb = xsb[i % 2]
        if i + 1 < NT:
            load(i + 1)
        emit_rmsnorm(C, x, xb, h, hb, gs, gsb, T, C.ones, sq, sqb, psb[7], ps[:, 7, 0:T], lnv, lnb, rstd, rsb)
        for j in range(NFC):
            bk = j % 3
            gp = ps[:, bk, 0:T]
            up = ps[:, bk, T:2 * T]
            mm_group(S, gp, [(w1s[:, k, j * 128:(j + 1) * 128], h[:, k, :]) for k in range(8)], [hb] + w1all, psb[bk])
            mm_group(S, up, [(w1s[:, k, DFF + j * 128:DFF + (j + 1) * 128], h[:, k, :]) for k in range(8)],
                     [hb] + w1all, psb[bk])
            s_ = sg[j % 2]; s_b = sgb[j % 2]
            S.op("act", lambda e, s_=s_, gp=gp: e.activation(out=s_[:], in_=gp, func=AF.Silu), reads=[psb[bk]], writes=[s_b])
            S.op("dve", lambda e, s_=s_, up=up, j=j: e.tensor_tensor(out=hm[:, j, :], in0=up, in1=s_[:], op=ALU.mult),
                 reads=[psb[bk], s_b], writes=[hmb])
        w2all = [b for p in w2b for b in p]
        for oc in range(8):
            bk = 3 + oc % 4
            yp = ps[:, bk, 0:T]
            mm_group(S, yp, [(w2s[:, f, oc * 128:(oc + 1) * 128], hm[:, f, :]) for f in range(NFC)], [hmb] + w2all, psb[bk])
            S.op("dve", lambda e, yp=yp, oc=oc, x=x: e.scalar_tensor_tensor(out=x[:, oc, :], in0=yp, scalar=0.5, in1=x[:, oc, :],
                                                                           op0=ALU.mult, op1=ALU.add),
                 reads=[psb[bk], xb], writes=[xb])
        S.dma("sp", xout[:, :, i * T:(i + 1) * T], x[:], xb, reads=[xb], writes=[outb])
    S.finish([outb] + xsb)
    S.run()
    C.close()


GELU_NATIVE = False
GC1 = 0.044715
GC2 = 1.5957691216057308


def phase_win(nc, x_d, w_d, g_d, qg_d, kg_d, bg_d, cos_d, sin_d, rot_d, q_o, k_o, v_o, xr_o, gy_o, gate_o, ntok):
    T = 512
    NT = ntok // T
    C = Ctx(nc)
    S = C.S
    const_setup(C)
    ws = C.sb([128, 8, INC], BF16, "ws")
    gs = C.sb([128, 8], F32, "gs"); gsb = S.buf("gs")
    qg = C.sb([128, 1], F32, "qg"); qgb = S.buf("qg")
    kg = C.sb([128, 1], F32, "kg"); kgb = S.buf("kg")
    bg = C.sb([128, 16], F32, "bg"); bgb = S.buf("bg")
    rot = C.sb([128, 128], BF16, "rot"); rotb = S.buf("rot")
    bones = C.sb([128, 128], BF16, "bones"); bonesb = S.buf("bones")
    S.dma("sp", gs[:], g_d, gsb, writes=[gsb])
    S.dma("sp", qg[:], qg_d, qgb, writes=[qgb])
    S.dma("sp", kg[:], kg_d, kgb, writes=[kgb])
    S.dma("sp", bg[:], bg_d, bgb, writes=[bgb])
    S.dma("pool", rot[:], rot_d, rotb, writes=[rotb])
    S.op("pool", lambda e: e.memset(bones[:], 0.0), writes=[bonesb])
    S.op("pool", lambda e: e.memset(bones[0:64, 0:64], 1.0), writes=[bonesb])
    S.op("pool", lambda e: e.memset(bones[64:128, 64:128], 1.0), writes=[bonesb])
    wb = load_weight(C, ws, w_d, 8, INC, "w", split=4)
    wall = [b for p in wb for b in p]

    x = C.sb([128, 8, T], F32, "x"); xb = S.buf("x")
    sq = C.sb([128, 8, T], BF16, "sq"); sqb = S.buf("sq")
    h = C.sb([128, 8, T], BF16, "h"); hb = S.buf("h")
    lnv = C.sb([128, T], F32, "lnv"); lnb = S.buf("lnv")
    rstd = C.sb([128, T], F32, "rstd"); rsb = S.buf("rstd")
    cs = C.sb([128, T], F32, "cs"); csb = S.buf("cs")
    sn = C.sb([128, T], F32, "sn"); snb = S.buf("sn")
    NR = 2
    sqc = [C.sb([128, T], BF16, "sqc%d" % i) for i in range(NR)]; sqcb = [S.buf("sqc%d" % i) for i in range(NR)]
    lnc = [C.sb([128, T], F32, "lnc%d" % i) for i in range(NR)]; lncb = [S.buf("lnc%d" % i) for i in range(NR)]
    rsc = [C.sb([128, T], F32, "rsc%d" % i) for i in range(NR)]; rscb = [S.buf("rsc%d" % i) for i in range(NR)]
    qn = [C.sb([128, T], BF16, "qn%d" % i) for i in range(NR)]; qnb = [S.buf("qn%d" % i) for i in range(NR)]
    t1 = [C.sb([128, T], F32, "t1%d" % i) for i in range(NR)]; t1b = [S.buf("t1%d" % i) for i in range(NR)]
    t2 = [C.sb([128, T], F32, "t2%d" % i) for i in range(NR)]; t2b = [S.buf("t2%d" % i) for i in range(NR)]
    qf = [C.sb([128, T], BF16, "qf%d" % i) for i in range(NR)]; qfb = [S.buf("qf%d" % i) for i in range(NR)]
    NSTG = 4
    stg = [C.sb([128, T], F32, "stg%d" % i) for i in range(NSTG)]; stgb = [S.buf("stg%d" % i) for i in range(NSTG)]
    tmp = [C.sb([128, T], F32, "tmp%d" % i) for i in range(2)]; tmpb = [S.buf("tmp%d" % i) for i in range(2)]
    vs = C.sb([128, 4, 256], BF16, "vs"); vsb = S.buf("vs")
    ps = C.psum
    psb = [S.buf("ps%d" % i) for i in range(8)]
    xin = feat(x_d)
    qo = feat(q_o); ko = feat(k_o); xro = feat(xr_o); gyo = feat(gy_o); gto = feat(gate_o)
    vo = v_o.rearrange("(n p) d -> p n d", p=128)
    outb = S.buf("out", par=True)
    stgi = [0]

    def proj(bank, col0):
        mm_group(S, ps[:, bank, :], [(ws[:, k, col0:col0 + 128], h[:, k, :]) for k in range(8)], [hb] + wall, psb[bank])

    for i in range(NT):
        tsl = slice(i * T, (i + 1) * T)
        S.dma("sp", x[:], xin[:, :, tsl], xb, writes=[xb])
        S.dma("sp", cs[:], cos_d[:, tsl], csb, writes=[csb])
        S.dma("sp", sn[:], sin_d[:, tsl], snb, writes=[snb])
        emit_rmsnorm(C, x, xb, h, hb, gs, gsb, T, C.ones, sq, sqb, psb[7], ps[:, 7, :], lnv, lnb, rstd, rsb)
        for c in range(10):
            r = c % NR
            bA, bB, bC = 3 * r, 3 * r + 1, 3 * r + 2
            gv, gvb = (qg, qgb) if c < 8 else (kg, kgb)
            proj(bA, c * 128)
            S.op("act", lambda e, r=r, bA=bA: e.activation(out=sqc[r][:], in_=ps[:, bA, :], func=AF.Square),
                 reads=[psb[bA]], writes=[sqcb[r]])
            mm_group(S, ps[:, bB, :], [(bones[:], sqc[r][:])], [sqcb[r], bonesb], psb[bB])
            S.op("act", lambda e, r=r, bB=bB: e.activation(out=lnc[r][:], in_=ps[:, bB, :], func=AF.Ln, scale=1.0 / HD, bias=C.eps[:]),
                 reads=[psb[bB], C.epsb], writes=[lncb[r]])
            S.op("act", lambda e, r=r: e.activation(out=rsc[r][:], in_=lnc[r][:], func=AF.Exp, scale=-0.5),
                 reads=[lncb[r]], writes=[rscb[r]])
            S.op("dve", lambda e, r=r, bA=bA, gv=gv: e.scalar_tensor_tensor(out=qn[r][:], in0=ps[:, bA, :], scalar=gv[:, 0:1], in1=rsc[r][:],
                                                                           op0=ALU.mult, op1=ALU.mult),
                 reads=[psb[bA], rscb[r], gvb], writes=[qnb[r]])
            mm_group(S, ps[:, bC, :], [(rot[:], qn[r][:])], [qnb[r], rotb], psb[bC])
            S.op("dve", lambda e, r=r: e.tensor_tensor(out=t1[r][:], in0=qn[r][:], in1=cs[:], op=ALU.mult),
                 reads=[qnb[r], csb], writes=[t1b[r]])
            S.op("dve", lambda e, r=r, bC=bC: e.tensor_tensor(out=t2[r][:], in0=ps[:, bC, :], in1=sn[:], op=ALU.mult),
                 reads=[psb[bC], snb], writes=[t2b[r]])
            S.op("dve", lambda e, r=r: e.tensor_tensor(out=qf[r][:], in0=t1[r][:], in1=t2[r][:], op=ALU.add),
                 reads=[t1b[r], t2b[r]], writes=[qfb[r]])
            dst = qo[:, c, tsl] if c < 8 else ko[:, c - 8, tsl]
            S.dma("pool", dst, qf[r][:], qfb[r], reads=[qfb[r]], writes=[outb])
        for tb in range(4):
            bank = 6 + tb % 2
            mm_group(S, ps[:, bank, 0:256], [(h[:, k, tb * 128:(tb + 1) * 128], ws[:, k, 1280:1536]) for k in range(8)],
                     [hb] + wall, psb[bank])
            S.op("act", lambda e, tb=tb, bank=bank: e.copy(out=vs[:, tb, :], in_=ps[:, bank, 0:256]), reads=[psb[bank]], writes=[vsb])
        S.dma("pool", vo[:, i * 4:(i + 1) * 4, :], vs[:], vsb, reads=[vsb], writes=[outb])
        for c in range(32):
            bank = 6 + c % 2
            si = stgi[0] % NSTG
            stgi[0] += 1
            st_, st_b = stg[si], stgb[si]
            proj(bank, 1536 + c * 128)
            pp = ps[:, bank, :]
            if c < 8:
                S.op("act", lambda e, st_=st_, pp=pp: e.copy(out=st_[:], in_=pp), reads=[psb[bank]], writes=[st_b])
                dst = xro[:, c, tsl]
            elif c < 16:
                if GELU_NATIVE:
                    S.op("act", lambda e, st_=st_, pp=pp: e.activation(out=st_[:], in_=pp, func=AF.Gelu_apprx_tanh),
                         reads=[psb[bank]], writes=[st_b])
                else:
                    ta, tab = tmp[0], tmpb[0]
                    tb_, tbb = tmp[1], tmpb[1]
                    S.op("act", lambda e, ta=ta, pp=pp: e.activation(out=ta[:], in_=pp, func=AF.Square), reads=[psb[bank]], writes=[tab])
                    S.op("dve", lambda e, ta=ta: e.tensor_scalar(out=ta[:], in0=ta[:], scalar1=GC1, scalar2=1.0, op0=ALU.mult, op1=ALU.add),
                         reads=[tab], writes=[tab])
                    S.op("dve", lambda e, ta=ta, tb_=tb_, pp=pp: e.tensor_tensor(out=tb_[:], in0=pp, in1=ta[:], op=ALU.mult),
                         reads=[tab, psb[bank]], writes=[tbb])
                    S.op("act", lambda e, tb_=tb_: e.activation(out=tb_[:], in_=tb_[:], func=AF.Sigmoid, scale=GC2), reads=[tbb], writes=[tbb])
                    S.op("dve", lambda e, st_=st_, tb_=tb_, pp=pp: e.tensor_tensor(out=st_[:], in0=pp, in1=tb_[:], op=ALU.mult),
                         reads=[tbb, psb[bank]], writes=[st_b])
                dst = gyo[:, c - 8, tsl]
            else:
                S.op("act", lambda e, st_=st_, pp=pp, c=c: e.activation(out=st_[:], in_=pp, func=AF.Sigmoid, bias=bg[:, c - 16:c - 15]),
                     reads=[psb[bank], bgb], writes=[st_b])
                dst = gto[:, c - 16, tsl]
            S.dma("pool", dst, st_[:], st_b, reads=[st_b], writes=[outb])
    S.finish([outb, xb, csb, snb])
    S.run()
    C.close()


def phase_att(nc, q_d, k_d, v_d, o_d, nq, nk):
    T = 512
    NQT = nq // T
    NKC = nk // 128
    NI = NKC // 2
    C = Ctx(nc)
    S = C.S
    kk = C.sb([128, 2, nk], BF16, "kk"); kkb = [S.buf("kk%d" % i) for i in range(4)]
    vv = C.sb([128, NKC, 4, 128], BF16, "vv")
    VSPL = max(1, NKC // 8)
    vvb = [S.buf("vv%d" % i) for i in range(NKC // VSPL)]
    for g in range(4):
        pb = (g // 2) * 64
        S.dma("sp", kk[pb:pb + 64, g % 2, :], k_d[g * 64:(g + 1) * 64, :], kkb[g], writes=[kkb[g]])
    vin = v_d.rearrange("(n p) d -> p n d", p=128)
    for s in range(NKC // VSPL):
        ks = slice(s * VSPL, (s + 1) * VSPL)
        S.op("pool", lambda e, ks=ks: e.memset(vv[:, ks, :, 64:128], 1.0), writes=[vvb[s]])
        for g in range(4):
            S.dma("sp", vv[:, ks, g, 0:64], vin[:, ks, g * 64:(g + 1) * 64], vvb[s], writes=[vvb[s]])
    qq = [C.sb([128, 8, T], BF16, "qq%d" % i) for i in range(2)]
    qqb = [S.buf("qq%d" % i) for i in range(2)]
    NP = 3
    P = [C.sb([128, 2, T], BF16, "P%d" % i) for i in range(NP)]
    Pb = [S.buf("P%d" % i) for i in range(NP)]
    rd = C.sb([64, T], F32, "rd"); rdb = S.buf("rd")
    ot = C.sb([64, 16, T], BF16, "ot"); otb = S.buf("ot")
    ps = C.psum
    psb = [S.buf("ps%d" % i) for i in range(8)]
    qin0 = q_d[0:512, :].rearrange("(j p) t -> p j t", p=64)
    qin1 = q_d[512:1024, :].rearrange("(j p) t -> p j t", p=64)
    oo = o_d.rearrange("(h p) t -> p h t", p=64)
    outb = S.buf("out", par=True)

    def loadq(i):
        tsl = slice(i * T, (i + 1) * T)
        S.dma("sp", qq[i % 2][0:64, :, :], qin0[:, :, tsl], qqb[i % 2], writes=[qqb[i % 2]])
        S.dma("sp", qq[i % 2][64:128, :, :], qin1[:, :, tsl], qqb[i % 2], writes=[qqb[i % 2]])

    its = [(i, h, n) for i in range(NQT) for h in range(16) for n in range(NI)]

    def emit_qk(idx):
        i, h, n = its[idx]
        pb = (h // 8) * 64
        g = h // 4
        pr = idx % 2
        for s in range(2):
            kc = 2 * n + s
            S.op("pe", lambda e, pr=pr, s=s, pb=pb, g=g, kc=kc, i=i, h=h: e.matmul(
                ps[:, 2 * pr + s, :], lhsT=kk[pb:pb + 64, g % 2, kc * 128:(kc + 1) * 128], rhs=qq[i % 2][pb:pb + 64, h % 8, :],
                start=True, stop=True),
                reads=[kkb[g], qqb[i % 2]], writes=[psb[2 * pr + s]], inc=(s == 1))

    loadq(0)
    emit_qk(0)
    for idx, (i, h, n) in enumerate(its):
        if h == 0 and n == 0 and i + 1 < NQT:
            loadq(i + 1)
        if idx + 1 < len(its):
            emit_qk(idx + 1)
        pr = idx % 2
        pi = idx % NP
        g = h // 4
        ob = 4 + h % 2
        S.op("act", lambda e, pr=pr, pi=pi: e.activation(out=P[pi][:], in_=ps[:, 2 * pr:2 * pr + 2, :], func=AF.Exp, scale=HD ** -0.5),
             reads=[psb[2 * pr], psb[2 * pr + 1]], writes=[Pb[pi]])
        for s in range(2):
            kc = 2 * n + s
            last = (kc == NKC - 1)
            S.op("pe", lambda e, ob=ob, kc=kc, g=g, pi=pi, s=s, last=last: e.matmul(
                ps[:, ob, :], lhsT=vv[:, kc, g, :], rhs=P[pi][:, s, :], start=(kc == 0), stop=last),
                reads=[Pb[pi], vvb[kc // VSPL]], writes=[psb[ob]], inc=last)
        if n == NI - 1:
            S.op("dve", lambda e, ob=ob: e.reciprocal(out=rd[:], in_=ps[64:128, ob, :]), reads=[psb[ob]], writes=[rdb])
            S.op("dve", lambda e, ob=ob, h=h: e.tensor_tensor(out=ot[:, h, :], in0=ps[0:64, ob, :], in1=rd[:], op=ALU.mult),
                 reads=[psb[ob], rdb], writes=[otb])
            if h == 15:
                S.dma("pool", oo[:, :, i * T:(i + 1) * T], ot[:], otb, reads=[otb], writes=[outb])
    S.finish([outb] + qqb)
    S.run()
    C.close()


A_CLAMP = 0.99999994


def phase_lru(nc, xr_d, gy_d, cw_d, cb_d, wa_d, wx_d, ba_d, bx_d, lam_d, m_o, xcs_d, hfs_d, ntok, NCH=4):
    T = 512
    NT = ntok // T
    C = Ctx(nc)
    S = C.S
    cst = C.sb([128, 2], F32, "cst"); cstb = S.buf("cst")
    S.op("pool", lambda e: e.memset(cst[:, 0:1], 1.0), writes=[cstb])
    cw = C.sb([128, NCH, 4], F32, "cw"); cwb = S.buf("cw")
    cb = C.sb([128, NCH], F32, "cb"); cbb = S.buf("cb")
    ba = C.sb([128, 2, NCH], F32, "ba"); bab = S.buf("ba")
    bx = C.sb([128, 2, NCH], F32, "bx"); bxb = S.buf("bx")
    lam = C.sb([128, 2, NCH], F32, "lam"); lamb = S.buf("lam")
    sc = C.sb([128, 2, NCH], F32, "sc"); scb = S.buf("sc")
    was = C.sb([128, 2, NCH, 128], BF16, "was"); wab = S.buf("was")
    wxs = C.sb([128, 2, NCH, 128], BF16, "wxs"); wxb = S.buf("wxs")
    S.dma("sp", cw[:], cw_d, cwb, writes=[cwb])
    S.dma("sp", cb[:], cb_d, cbb, writes=[cbb])
    S.dma("sp", ba[:], ba_d, bab, writes=[bab])
    S.dma("sp", bx[:], bx_d, bxb, writes=[bxb])
    S.dma("sp", lam[:], lam_d, lamb, writes=[lamb])
    S.dma("pool", was[:], wa_d.rearrange("d c p m -> p d c m"), wab, writes=[wab])
    S.dma("pool", wxs[:], wx_d.rearrange("d c p m -> p d c m"), wxb, writes=[wxb])
    S.op("act", lambda e: e.activation(out=sc[:], in_=lam[:], func=AF.Exp, scale=-1.0), reads=[lamb], writes=[scb])
    S.op("act", lambda e: e.activation(out=sc[:], in_=sc[:], func=AF.Ln, bias=cst[:, 0:1]), reads=[scb, cstb], writes=[scb])
    S.op("dve", lambda e: e.tensor_scalar(out=sc[:], in0=sc[:], scalar1=-8.0, scalar2=None, op0=ALU.mult), reads=[scb], writes=[scb])

    NR = 2
    def mk(name, dt=F32, w=T):
        return ([C.sb([128, w], dt, "%s%d" % (name, i)) for i in range(NR)], [S.buf("%s%d" % (name, i)) for i in range(NR)])
    xw = C.sb([128, NCH, T + 3], F32, "xw"); xwb = S.buf("xw")
    xc, xcb_ = mk("xc"); xb16, xb16b = mk("xb16", BF16)
    rr, rrb = mk("rr"); ii, iib = mk("ii"); aa, aab = mk("aa"); a2, a2b = mk("a2"); uu, uub = mk("uu")
    hh, hhb = mk("hh"); hf, hfb = mk("hf"); gy, gyb = mk("gy"); mo, mob = mk("mo", BF16)
    car = C.sb([128, NCH], F32, "car"); carb = [S.buf("car%d" % c) for c in range(NCH)]
    ps = C.psum
    psb = [S.buf("ps%d" % i) for i in range(8)]
    xrv = feat(xr_d); gyv = feat(gy_d); mv = feat(m_o); xcv = feat(xcs_d); hfv = feat(hfs_d)
    xcsb = S.buf("xcs", par=True); hfsb = S.buf("hfs", par=True); outb = S.buf("out", par=True)
    cnt = [0]

    def gates(d, c, r, reverse, first):
        bk = 2 * (cnt[0] % 4)
        cnt[0] += 1
        S.op("act", lambda e: e.copy(out=xb16[r][:], in_=xc[r][:]), reads=[xcb_[r]], writes=[xb16b[r]])
        mm_group(S, ps[:, bk, :], [(was[:, d, c, :], xb16[r][:])], [xb16b[r], wab], psb[bk])
        mm_group(S, ps[:, bk + 1, :], [(wxs[:, d, c, :], xb16[r][:])], [xb16b[r], wxb], psb[bk + 1])
        S.op("act", lambda e: e.activation(out=rr[r][:], in_=ps[:, bk, :], func=AF.Sigmoid, bias=ba[:, d, c:c + 1]),
             reads=[psb[bk], bab], writes=[rrb[r]])
        S.op("act", lambda e: e.activation(out=ii[r][:], in_=ps[:, bk + 1, :], func=AF.Sigmoid, bias=bx[:, d, c:c + 1]),
             reads=[psb[bk + 1], bxb], writes=[iib[r]])
        S.op("act", lambda e: e.activation(out=aa[r][:], in_=rr[r][:], func=AF.Exp, scale=sc[:, d, c:c + 1]),
             reads=[rrb[r], scb], writes=[aab[r]])
        S.op("dve", lambda e: e.scalar_tensor_tensor(out=a2[r][:], in0=aa[r][:], scalar=A_CLAMP, in1=aa[r][:], op0=ALU.min, op1=ALU.mult),
             reads=[aab[r]], writes=[a2b[r]])
        S.op("act", lambda e: e.activation(out=a2[r][:], in_=a2[r][:], func=AF.Ln, scale=-1.0, bias=cst[:, 0:1]),
             reads=[a2b[r], cstb], writes=[a2b[r]])
        S.op("act", lambda e: e.activation(out=a2[r][:], in_=a2[r][:], func=AF.Exp, scale=0.5), reads=[a2b[r]], writes=[a2b[r]])
        S.op("dve", lambda e: e.tensor_tensor(out=uu[r][:], in0=xc[r][:], in1=ii[r][:], op=ALU.mult),
             reads=[xcb_[r], iib[r]], writes=[uub[r]])
        S.op("dve", lambda e: e.tensor_tensor(out=uu[r][:], in0=uu[r][:], in1=a2[r][:], op=ALU.mult),
             reads=[uub[r], a2b[r]], writes=[uub[r]])
        init = 0.0 if first else car[:, c:c + 1]
        rd_ = [aab[r], uub[r]] + ([] if first else [carb[c]])
        if not reverse:
            S.op("dve", lambda e: e.tensor_tensor_scan(out=hh[r][:], data0=aa[r][:], data1=uu[r][:], initial=init, op0=ALU.mult, op1=ALU.add),
                 reads=rd_, writes=[hhb[r]])
            S.op("act", lambda e: e.copy(out=car[:, c:c + 1], in_=hh[r][:, T - 1:T]), reads=[hhb[r]], writes=[carb[c]])
        else:
            S.op("dve", lambda e: e.tensor_tensor_scan(out=hh[r][:, ::-1], data0=aa[r][:, ::-1], data1=uu[r][:, ::-1], initial=init,
                                                        op0=ALU.mult, op1=ALU.add),
                 reads=rd_, writes=[hhb[r]])
            S.op("act", lambda e: e.copy(out=car[:, c:c + 1], in_=hh[r][:, 0:1]), reads=[hhb[r]], writes=[carb[c]])

    k = 0
    for i in range(NT):
        t0 = i * T
        lo = max(t0 - 2, 0); hi = min(t0 + T + 1, ntok)
        if i == 0:
            S.op("pool", lambda e: e.memset(xw[:, :, 0:2], 0.0), writes=[xwb])
        if i == NT - 1:
            S.op("pool", lambda e: e.memset(xw[:, :, T + 2:T + 3], 0.0), writes=[xwb])
        S.dma("sp", xw[:, :, lo - (t0 - 2):hi - (t0 - 2)], xrv[:, :, lo:hi], xwb, writes=[xwb])
        for c in range(NCH):
            r = k % NR; k += 1
            S.op("dve", lambda e, c=c, r=r: e.tensor_scalar(out=xc[r][:], in0=xw[:, c, 0:T], scalar1=cw[:, c, 0:1], scalar2=cb[:, c:c + 1],
                                                             op0=ALU.mult, op1=ALU.add),
                 reads=[xwb, cwb, cbb], writes=[xcb_[r]])
            for j in range(1, 4):
                S.op("dve", lambda e, c=c, r=r, j=j: e.scalar_tensor_tensor(out=xc[r][:], in0=xw[:, c, j:j + T], scalar=cw[:, c, j:j + 1],
                                                                              in1=xc[r][:], op0=ALU.mult, op1=ALU.add),
                     reads=[xwb, cwb, xcb_[r]], writes=[xcb_[r]])
            S.dma("pool", xcv[:, c, t0:t0 + T], xc[r][:], xcb_[r], reads=[xcb_[r]], writes=[xcsb])
            gates(0, c, r, False, i == 0)
            S.dma("pool", hfv[:, c, t0:t0 + T], hh[r][:], hhb[r], reads=[hhb[r]], writes=[hfsb])
    for i in range(NT - 1, -1, -1):
        t0 = i * T
        for c in range(NCH):
            r = k % NR; k += 1
            S.dma("sp", xc[r][:], xcv[:, c, t0:t0 + T], xcb_[r], reads=[xcsb], writes=[xcb_[r]])
            S.dma("sp", hf[r][:], hfv[:, c, t0:t0 + T], hfb[r], reads=[hfsb], writes=[hfb[r]])
            S.dma("sp", gy[r][:], gyv[:, c, t0:t0 + T], gyb[r], writes=[gyb[r]])
            gates(1, c, r, True, i == NT - 1)
            S.op("dve", lambda e, r=r: e.tensor_tensor(out=hf[r][:], in0=hf[r][:], in1=hh[r][:], op=ALU.add),
                 reads=[hfb[r], hhb[r]], writes=[hfb[r]])
            S.op("dve", lambda e, r=r: e.tensor_tensor(out=mo[r][:], in0=hf[r][:], in1=gy[r][:], op=ALU.mult),
                 reads=[hfb[r], gyb[r]], writes=[mob[r]])
            S.dma("pool", mv[:, c, t0:t0 + T], mo[r][:], mob[r], reads=[mob[r]], writes=[outb])
    S.finish([outb, xcsb, hfsb])
    S.run()
    C.close()


def phase_mix(nc, o_d, m_d, gate_d, x1_d, wao_d, wro_d, wout_d, out_d, ntok):
    T = 512
    NT = ntok // T
    C = Ctx(nc)
    S = C.S
    wao = C.sb([128, 8, D], BF16, "wao"); wro = C.sb([128, 8, D], BF16, "wro"); wout = C.sb([128, 8, D], BF16, "wout")
    waob = [b for p in load_weight(C, wao, wao_d, 8, D, "wao", split=1) for b in p]
    wrob = [b for p in load_weight(C, wro, wro_d, 8, D, "wro", split=1) for b in p]
    woutb = [b for p in load_weight(C, wout, wout_d, 8, D, "wout", split=1) for b in p]
    o = C.sb([128, 8, T], BF16, "o"); ob = S.buf("o")
    m = C.sb([128, 8, T], BF16, "m"); mb = S.buf("m")
    gt = C.sb([128, 16, T], F32, "gt"); gtb = S.buf("gt")
    x1 = [C.sb([128, 8, T], F32, "x1%d" % i) for i in range(2)]; x1b = [S.buf("x1%d" % i) for i in range(2)]
    tA = [C.sb([128, T], F32, "tA%d" % i) for i in range(2)]; tAb = [S.buf("tA%d" % i) for i in range(2)]
    tB = [C.sb([128, T], F32, "tB%d" % i) for i in range(2)]; tBb = [S.buf("tB%d" % i) for i in range(2)]
    mg = C.sb([128, 8, T], BF16, "mg"); mgb = S.buf("mg")
    ps = C.psum
    psb = [S.buf("ps%d" % i) for i in range(8)]
    ov = feat(o_d); mv = feat(m_d); gv = feat(gate_d); xv = feat(x1_d); outv = feat(out_d)
    outb = S.buf("out", par=True)
    for i in range(NT):
        tsl = slice(i * T, (i + 1) * T)
        x = x1[i % 2]; xb = x1b[i % 2]
        S.dma("sp", o[:], ov[:, :, tsl], ob, writes=[ob])
        S.dma("sp", m[:], mv[:, :, tsl], mb, writes=[mb])
        S.dma("sp", gt[:], gv[:, :, tsl], gtb, writes=[gtb])
        S.dma("sp", x[:], xv[:, :, tsl], xb, writes=[xb])
        for oc in range(8):
            r = oc % 2
            bA, bB = 2 * r, 2 * r + 1
            osl = slice(oc * 128, (oc + 1) * 128)
            mm_group(S, ps[:, bA, :], [(wao[:, c, osl], o[:, c, :]) for c in range(8)], [ob] + waob, psb[bA])
            mm_group(S, ps[:, bB, :], [(wro[:, c, osl], m[:, c, :]) for c in range(8)], [mb] + wrob, psb[bB])
            S.op("dve", lambda e, r=r, bA=bA, oc=oc: e.tensor_tensor(out=tA[r][:], in0=ps[:, bA, :], in1=gt[:, oc, :], op=ALU.mult),
                 reads=[psb[bA], gtb], writes=[tAb[r]])
            S.op("dve", lambda e, r=r, bB=bB, oc=oc: e.tensor_tensor(out=tB[r][:], in0=ps[:, bB, :], in1=gt[:, 8 + oc, :], op=ALU.mult),
                 reads=[psb[bB], gtb], writes=[tBb[r]])
            S.op("dve", lambda e, r=r, oc=oc: e.tensor_tensor(out=mg[:, oc, :], in0=tA[r][:], in1=tB[r][:], op=ALU.add),
                 reads=[tAb[r], tBb[r]], writes=[mgb])
        for oc in range(8):
            bk = 4 + oc % 4
            osl = slice(oc * 128, (oc + 1) * 128)
            mm_group(S, ps[:, bk, :], [(wout[:, c, osl], mg[:, c, :]) for c in range(8)], [mgb] + woutb, psb[bk])
            S.op("dve", lambda e, bk=bk, oc=oc, x=x: e.tensor_tensor(out=x[:, oc, :], in0=ps[:, bk, :], in1=x[:, oc, :], op=ALU.add),
                 reads=[psb[bk], xb], writes=[xb])
        S.dma("pool", outv[:, :, tsl], x[:], xb, reads=[xb], writes=[outb])
    S.finish([outb, ob, mb, gtb] + x1b)
    S.run()
    C.close()


NTOK = SEQ // 2
BF = ml_dtypes.bfloat16


def _din(nc, n, s, dt=F32):
    return nc.dram_tensor(n, list(s), dt, kind="ExternalInput").ap()


def _dout(nc, n, s, dt=F32):
    return nc.dram_tensor(n, list(s), dt, kind="ExternalOutput").ap()


def _dint(nc, n, s, dt=F32):
    return nc.dram_tensor(n, list(s), dt, kind="Internal").ap()


def build_A(ntok=NTOK):
    nc = bass.Bass("TRN2", target_bir_lowering=False)
    xT = _din(nc, "xT", [D, ntok]); w1 = _din(nc, "w1", [D, 2 * DFF]); w2 = _din(nc, "w2", [DFF, D]); g1 = _din(nc, "g1", [128, 8])
    win = _din(nc, "win", [D, INC]); g2 = _din(nc, "g2", [128, 8]); qg = _din(nc, "qg", [128, 1]); kg = _din(nc, "kg", [128, 1])
    bg = _din(nc, "bg", [128, 16]); cos = _din(nc, "cos", [128, ntok]); sin = _din(nc, "sin", [128, ntok]); rot = _din(nc, "rot", [128, 128])
    x1T = _dout(nc, "x1T", [D, ntok]); q = _dout(nc, "q", [D, ntok], BF16); k = _dout(nc, "k", [256, ntok], BF16)
    v = _dout(nc, "v", [ntok, 256], BF16); xr = _dout(nc, "xr", [D, ntok]); gy = _dout(nc, "gy", [D, ntok]); gate = _dout(nc, "gate", [2 * D, ntok])
    phase_ffn(nc, xT, x1T, w1, w2, g1, ntok)
    phase_win(nc, x1T, win, g2, qg, kg, bg, cos, sin, rot, q, k, v, xr, gy, gate, ntok)
    return nc


def build_BL(nq=NTOK, nk=SEQ):
    nc = bass.Bass("TRN2", target_bir_lowering=False)
    q = _din(nc, "q", [D, nq], BF16); k = _din(nc, "k", [256, nk], BF16); v = _din(nc, "v", [nk, 256], BF16)
    xr = _din(nc, "xr", [512, nk]); gy = _din(nc, "gy", [512, nk]); cw = _din(nc, "cw", [128, 4, 4]); cb = _din(nc, "cb", [128, 4])
    wa = _din(nc, "wa", [2, 4, 128, 128]); wx = _din(nc, "wx", [2, 4, 128, 128])
    ba = _din(nc, "ba", [128, 2, 4]); bx = _din(nc, "bx", [128, 2, 4]); lam = _din(nc, "lam", [128, 2, 4])
    o = _dout(nc, "o", [D, nq], BF16); m = _dout(nc, "m", [512, nk], BF16)
    xcs = _dint(nc, "xcs", [512, nk]); hfs = _dint(nc, "hfs", [512, nk])
    phase_lru(nc, xr, gy, cw, cb, wa, wx, ba, bx, lam, m, xcs, hfs, nk, 4)
    phase_att(nc, q, k, v, o, nq, nk)
    return nc


def build_C(ntok=NTOK):
    nc = bass.Bass("TRN2", target_bir_lowering=False)
    o = _din(nc, "o", [D, ntok], BF16); m = _din(nc, "m", [D, ntok], BF16); gate = _din(nc, "gate", [2 * D, ntok]); x1T = _din(nc, "x1T", [D, ntok])
    wao = _din(nc, "wao", [D, D]); wro = _din(nc, "wro", [D, D]); wout = _din(nc, "wout", [D, D])
    w1 = _din(nc, "w1", [D, 2 * DFF]); w2 = _din(nc, "w2", [DFF, D]); g3 = _din(nc, "g3", [128, 8])
    x2T = _dint(nc, "x2T", [D, ntok]); x3T = _dout(nc, "x3T", [D, ntok])
    phase_mix(nc, o, m, gate, x1T, wao, wro, wout, x2T, ntok)
    phase_ffn(nc, x2T, x3T, w1, w2, g3, ntok)
    return nc


def rope_tables(pos):
    inv = (ROPE_THETA ** (-np.arange(0, 32, 2, dtype=np.float32) / np.float32(32))).astype(np.float32)
    row = (pos // GRID_W).astype(np.float32)
    col = (pos % GRID_W).astype(np.float32)
    cc = np.zeros((128, len(pos)), np.float32)
    ss = np.zeros((128, len(pos)), np.float32)
    for p in range(128):
        f = p % 64
        ang = ((row if f < 32 else col) * inv[f % 16]).astype(np.float32)
        cc[p] = np.cos(ang)
        ss[p] = np.sin(ang) * (-1.0 if (f % 32) < 16 else 1.0)
    return cc, ss


def rot_matrix():
    r = np.zeros((128, 128), np.float32)
    for m_ in range(128):
        f = m_ % 64
        pf = f + 16 if (f % 32) < 16 else f - 16
        r[m_ - f + pf, m_] = 1.0
    return r


def _pc(vec):
    return np.ascontiguousarray(vec.reshape(-1, 128).T)


def _bd(w):
    o = np.zeros((2, 4, 128, 128), np.float32)
    for d in range(2):
        for c in range(4):
            o[d, c, :64, :64] = w[d, 2 * c]
            o[d, c, 64:, 64:] = w[d, 2 * c + 1]
    return o


def _run(nc, in_maps):
    res = _bu.run_bass_kernel_spmd(nc, in_maps, core_ids=list(range(len(in_maps))))
    return res.results


LAYER_W = ["w1a", "w2a", "g1", "win", "g2", "qg", "kg", "bg", "wao", "wro", "wout", "w1b", "w2b", "g3",
           "cw", "cb", "wa", "wx", "ba", "bx", "lam"]


def build_fused(seq=SEQ, depth=2):
    nc = bass.Bass("TRN2", target_bir_lowering=False)
    shp = {"w1a": [D, 2 * DFF], "w2a": [DFF, D], "g1": [128, 8], "win": [D, INC], "g2": [128, 8], "qg": [128, 1], "kg": [128, 1],
           "bg": [128, 16], "wao": [D, D], "wro": [D, D], "wout": [D, D], "w1b": [D, 2 * DFF], "w2b": [DFF, D], "g3": [128, 8],
           "cw": [128, 8, 4], "cb": [128, 8], "wa": [2, 8, 128, 128], "wx": [2, 8, 128, 128], "ba": [128, 2, 8], "bx": [128, 2, 8],
           "lam": [128, 2, 8]}
    xT = _din(nc, "xT", [D, seq])
    cos = _din(nc, "cos", [128, seq]); sin = _din(nc, "sin", [128, seq]); rot = _din(nc, "rot", [128, 128])
    W = [{n: _din(nc, "%s_%d" % (n, l), shp[n]) for n in LAYER_W} for l in range(depth)]
    outT = _dout(nc, "outT", [D, seq])
    x1 = _dint(nc, "s_x1", [D, seq]); x2 = _dint(nc, "s_x2", [D, seq]); x3 = _dint(nc, "s_x3", [D, seq])
    q = _dint(nc, "s_q", [D, seq], BF16); k = _dint(nc, "s_k", [256, seq], BF16); v = _dint(nc, "s_v", [seq, 256], BF16)
    xr = _dint(nc, "s_xr", [D, seq]); gy = _dint(nc, "s_gy", [D, seq]); gate = _dint(nc, "s_gate", [2 * D, seq])
    o = _dint(nc, "s_o", [D, seq], BF16); m = _dint(nc, "s_m", [D, seq], BF16)
    xcs = _dint(nc, "s_xcs", [D, seq]); hfs = _dint(nc, "s_hfs", [D, seq])
    xin = xT
    for l in range(depth):
        w = W[l]
        xo = outT if l == depth - 1 else x3
        phase_ffn(nc, xin, x1, w["w1a"], w["w2a"], w["g1"], seq)
        phase_win(nc, x1, w["win"], w["g2"], w["qg"], w["kg"], w["bg"], cos, sin, rot, q, k, v, xr, gy, gate, seq)
        phase_lru(nc, xr, gy, w["cw"], w["cb"], w["wa"], w["wx"], w["ba"], w["bx"], w["lam"], m, xcs, hfs, seq, 8)
        phase_att(nc, q, k, v, o, seq, seq)
        phase_mix(nc, o, m, gate, x1, w["wao"], w["wro"], w["wout"], x2, seq)
        phase_ffn(nc, x2, xo, w["w1b"], w["w2b"], w["g3"], seq)
        xin = x3
    return nc


def _bd8(w):
    o = np.zeros((2, 8, 128, 128), np.float32)
    for d in range(2):
        for c in range(8):
            o[d, c, :64, :64] = w[d, 2 * c]
            o[d, c, 64:, 64:] = w[d, 2 * c + 1]
    return o


def layer_inputs(l, p):
    f = lambda a: np.ascontiguousarray(np.asarray(a, dtype=np.float32))
    t3 = lambda a: np.ascontiguousarray(f(a).reshape(2, 8, 128).transpose(2, 0, 1))
    d = {"w1a": f(p["ffn1_w1"][l]), "w2a": f(p["ffn1_w2"][l]), "g1": _pc(f(p["ffn1_norm"][l])), "win": f(p["w_in"][l]),
         "g2": _pc(f(p["mix_norm"][l])), "qg": np.tile(f(p["q_norm"][l]), 2)[:, None].copy(), "kg": np.tile(f(p["k_norm"][l]), 2)[:, None].copy(),
         "bg": _pc(f(p["b_gate"][l]).reshape(-1)), "wao": f(p["w_attn_o"][l]), "wro": f(p["w_rnn_o"][l]), "wout": f(p["w_out"][l]),
         "w1b": f(p["ffn2_w1"][l]), "w2b": f(p["ffn2_w2"][l]), "g3": _pc(f(p["ffn2_norm"][l])),
         "cw": np.ascontiguousarray(f(p["conv_w"][l]).reshape(4, 8, 128).transpose(2, 1, 0)), "cb": _pc(f(p["conv_b"][l])),
         "wa": _bd8(f(p["lru_wa"][l])), "wx": _bd8(f(p["lru_wx"][l])), "ba": t3(p["lru_ba"][l]), "bx": t3(p["lru_bx"][l]),
         "lam": t3(p["lru_lambda"][l])}
    return {"%s_%d" % (n, l): a for n, a in d.items()}


def kernel(**p):
    x = np.ascontiguousarray(np.asarray(p["x"], dtype=np.float32))
    nb, seq, _ = x.shape
    depth = np.asarray(p["ffn1_norm"]).shape[0]
    nc = build_fused(seq, depth)
    cc, ss = rope_tables(np.arange(seq))
    common = {"cos": cc, "sin": ss, "rot": rot_matrix()}
    for l in range(depth):
        common.update(layer_inputs(l, p))
    in_maps = [dict(common, xT=np.ascontiguousarray(x[b].T)) for b in range(nb)]
    res = _run(nc, in_maps)
    out = np.empty_like(x)
    for b in range(nb):
        out[b] = res[b]["outT"].T
    return out
```

```python
import numpy as np
import ml_dtypes
from contextlib import ExitStack
import concourse.bass as bass
import concourse.mybir as mybir
import concourse.bass_utils as _bu

F32 = mybir.dt.float32
BF16 = mybir.dt.bfloat16
AF = mybir.ActivationFunctionType
ALU = mybir.AluOpType

D = 1024
DFF = 2816
NFC = DFF // 128
BATCH = 4
SEQ = 8192
NH = 16
NKV = 4
HD = 64
INC = 5632
EPS = 1e-6
GRID_W = 64
ROPE_THETA = 10000.0


class Buf:
    __slots__ = ("name", "last_w", "readers", "sem", "dcount", "par")

    def __init__(self, name, par=False):
        self.name = name
        self.last_w = {}
        self.readers = {}
        self.sem = None
        self.dcount = 0
        self.par = par


class Sched:
    ENG = ("pe", "dve", "act", "pool", "sp")

    def __init__(self, nc, es, tag=""):
        self.nc = nc
        self.es = es
        self.tag = tag
        self.ops = {e: [] for e in self.ENG}
        self.cnt = {e: 0 for e in self.ENG}
        self.sem = {e: nc.alloc_semaphore(name=tag + "c_" + e) for e in self.ENG}
        self.seen = {e: {} for e in self.ENG}
        self.nsem = 5
        self.nbuf = 0

    def buf(self, name=None, par=False):
        self.nbuf += 1
        return Buf(name or ("b%d" % self.nbuf), par)

    def _wait(self, eng, deps):
        seen = self.seen[eng]
        for key, n in deps.items():
            if key[0] == "c" and key[1] == eng and eng == "pe":
                continue
            if seen.get(key, 0) >= n:
                continue
            seen[key] = n
            if key[0] == "c":
                sem, val = self.sem[key[1]], n
            else:
                sem, val = key[1], 16 * n
            self.ops[eng].append(lambda e, sem=sem, val=val: e.wait_ge(sem, val))

    @staticmethod
    def _merge(dst, src):
        for k, n in src.items():
            if dst.get(k, 0) < n:
                dst[k] = n

    def _deps(self, reads, writes):
        deps = {}
        for b in reads:
            self._merge(deps, b.last_w)
        for b in writes:
            self._merge(deps, b.last_w)
            self._merge(deps, b.readers)
        return deps

    def _commit(self, tok, reads, writes):
        key, n = tok
        for b in writes:
            if b.par:
                if b.last_w.get(key, 0) < n:
                    b.last_w[key] = n
            else:
                b.last_w = {key: n}
                b.readers = {}
        for b in reads:
            if b.readers.get(key, 0) < n:
                b.readers[key] = n

    def op(self, eng, fn, reads=(), writes=(), inc=True):
        self._wait(eng, self._deps(reads, writes))
        if inc:
            self.cnt[eng] += 1
            sem = self.sem[eng]
            self.ops[eng].append(lambda e, fn=fn, sem=sem: fn(e).then_inc(sem, 1))
            tok = (("c", eng), self.cnt[eng])
        else:
            self.ops[eng].append(lambda e, fn=fn: fn(e))
            tok = (("c", eng), self.cnt[eng] + 1)
        self._commit(tok, reads, writes)

    def dma(self, eng, out, in_, sb, reads=(), writes=()):
        self._wait(eng, self._deps(reads, writes))
        if sb.sem is None:
            sb.sem = {}
            sb.dcount = {}
        if eng not in sb.sem:
            sb.sem[eng] = self.nc.alloc_semaphore(name="%sd%s_%s" % (self.tag, eng, sb.name))
            sb.dcount[eng] = 0
            self.nsem += 1
        sb.dcount[eng] += 1
        sem = sb.sem[eng]
        cnt_ = sb.dcount[eng]
        self.ops[eng].append(lambda e, out=out, in_=in_, sem=sem: e.dma_start(out=out, in_=in_).then_inc(sem, 16))
        tok = (("d", sem), cnt_)
        self._commit(tok, reads, writes)

    def finish(self, bufs, eng="sp"):
        deps = {}
        for b in bufs:
            self._merge(deps, b.last_w)
            self._merge(deps, b.readers)
        self._wait(eng, deps)

    def run(self):
        ops = self.ops
        with self.nc.Block() as block:
            @block.tensor
            def _(e):
                for f in ops["pe"]:
                    f(e)

            @block.vector
            def _(e):
                for f in ops["dve"]:
                    f(e)

            @block.scalar
            def _(e):
                for f in ops["act"]:
                    f(e)

            @block.gpsimd
            def _(e):
                for f in ops["pool"]:
                    f(e)

            @block.sync
            def _(e):
                for f in ops["sp"]:
                    f(e)


_PH = [0]


class Ctx:
    def __init__(self, nc):
        self.nc = nc
        self.cleanup = nc.cleanup_on_exit()
        self.cleanup.__enter__()
        self.es = ExitStack()
        self.tag = "p%d" % _PH[0]
        _PH[0] += 1
        self.S = Sched(nc, self.es, self.tag)
        self.psum = self.es.enter_context(nc.psum_tensor(self.tag + "ps", [128, 8, 512], F32))
        self.n = 0

    def sb(self, shape, dtype, name=None):
        self.n += 1
        return self.es.enter_context(self.nc.sbuf_tensor(self.tag + "sb_" + (name or ("t%d" % self.n)), shape, dtype))

    def close(self):
        self.es.close()
        self.cleanup.__exit__(None, None, None)


def feat(ap):
    return ap.rearrange("(c p) t -> p c t", p=128)


def mm_group(S, out, pairs, reads, wbuf):
    n = len(pairs)
    for i, (l, r) in enumerate(pairs):
        S.op("pe", lambda e, l=l, r=r, i=i: e.matmul(out, lhsT=l, rhs=r, start=(i == 0), stop=(i == n - 1)),
             reads=reads, writes=[wbuf], inc=(i == n - 1))


def emit_rmsnorm(C, x, xb, h, hb, gs, gsb, T, ones, sq, sqb, ssb, ps_ss, lnv, lnb, rstd, rsb, nfeat=1024):
    S = C.S
    S.op("act", lambda e: e.activation(out=sq[:], in_=x[:], func=AF.Square), reads=[xb], writes=[sqb])
    mm_group(S, ps_ss, [(ones[:], sq[:, c, :]) for c in range(8)], [sqb, C.onesb], ssb)
    S.op("act", lambda e: e.activation(out=lnv[:], in_=ps_ss, func=AF.Ln, scale=1.0 / nfeat, bias=C.eps[:]),
         reads=[ssb, C.epsb], writes=[lnb])
    S.op("act", lambda e: e.activation(out=rstd[:], in_=lnv[:], func=AF.Exp, scale=-0.5), reads=[lnb], writes=[rsb])
    for c in range(8):
        S.op("dve", lambda e, c=c: e.scalar_tensor_tensor(out=h[:, c, :], in0=x[:, c, :], scalar=gs[:, c:c + 1],
                                                          in1=rstd[:], op0=ALU.mult, op1=ALU.mult),
             reads=[xb, rsb, gsb], writes=[hb])


def load_weight(C, wsb, w_d, nk, ncols, name, split=2):
    S = C.S
    wv = w_d.rearrange("(k p) n -> p k n", p=128)
    bufs = []
    step = (ncols + split - 1) // split
    for k in range(nk):
        pieces = []
        for s0 in range(0, ncols, step):
            b = S.buf("%s%d_%d" % (name, k, s0))
            s1 = min(ncols, s0 + step)
            S.dma("pool", wsb[:, k, s0:s1], wv[:, k, s0:s1], b, writes=[b])
            pieces.append(b)
        bufs.append(pieces)
    return bufs


def const_setup(C):
    S = C.S
    C.ones = C.sb([128, 128], BF16, "ones")
    C.onesb = S.buf("ones")
    C.eps = C.sb([128, 1], F32, "eps")
    C.epsb = S.buf("eps")
    S.op("pool", lambda e: e.memset(C.ones[:], 1.0), writes=[C.onesb])
    S.op("pool", lambda e: e.memset(C.eps[:], EPS), writes=[C.epsb])


def phase_ffn(nc, x_d, out_d, w1_d, w2_d, g_d, ntok):
    T = 256
    NT = ntok // T
    C = Ctx(nc)
    S = C.S
    const_setup(C)
    w1s = C.sb([128, 8, 2 * DFF], BF16, "w1s")
    w2s = C.sb([128, NFC, D], BF16, "w2s")
    gs = C.sb([128, 8], F32, "gs")
    gsb = S.buf("gs")
    S.dma("sp", gs[:], g_d, gsb, writes=[gsb])
    w1b = load_weight(C, w1s, w1_d, 8, 2 * DFF, "w1", split=2)
    w2b = load_weight(C, w2s, w2_d, NFC, D, "w2", split=1)
    w1all = [b for p in w1b for b in p]
    xs = [C.sb([128, 8, T], F32, "x%d" % i) for i in range(2)]
    xsb = [S.buf("x%d" % i) for i in range(2)]
    sq = C.sb([128, 8, T], BF16, "sq"); sqb = S.buf("sq")
    h = C.sb([128, 8, T], BF16, "h"); hb = S.buf("h")
    lnv = C.sb([128, T], F32, "lnv"); lnb = S.buf("lnv")
    rstd = C.sb([128, T], F32, "rstd"); rsb = S.buf("rstd")
    hm = C.sb([128, NFC, T], BF16, "hm"); hmb = S.buf("hm")
    sg = [C.sb([128, T], F32, "sg%d" % i) for i in range(2)]
    sgb = [S.buf("sg%d" % i) for i in range(2)]
    ps = C.psum
    psb = [S.buf("ps%d" % i) for i in range(8)]
    xin = feat(x_d)
    xout = feat(out_d)
    outb = S.buf("out", par=True)

    def load(i):
        S.dma("sp", xs[i % 2][:], xin[:, :, i * T:(i + 1) * T], xsb[i % 2], writes=[xsb[i % 2]])

    load(0)
    for i in range(NT):
        x = xs[i % 2]; xb = xsb[i % 2]
        if i + 1 < NT:
            load(i + 1)
        emit_rmsnorm(C, x, xb, h, hb, gs, gsb, T, C.ones, sq, sqb, psb[7], ps[:, 7, 0:T], lnv, lnb, rstd, rsb)
        for j in range(NFC):
            bk = j % 3
            gp = ps[:, bk, 0:T]
            up = ps[:, bk, T:2 * T]
            mm_group(S, gp, [(w1s[:, k, j * 128:(j + 1) * 128], h[:, k, :]) for k in range(8)], [hb] + w1all, psb[bk])
            mm_group(S, up, [(w1s[:, k, DFF + j * 128:DFF + (j + 1) * 128], h[:, k, :]) for k in range(8)],
                     [hb] + w1all, psb[bk])
            s_ = sg[j % 2]; s_b = sgb[j % 2]
            S.op("act", lambda e, s_=s_, gp=gp: e.activation(out=s_[:], in_=gp, func=AF.Silu), reads=[psb[bk]], writes=[s_b])
            S.op("dve", lambda e, s_=s_, up=up, j=j: e.tensor_tensor(out=hm[:, j, :], in0=up, in1=s_[:], op=ALU.mult),
                 reads=[psb[bk], s_b], writes=[hmb])
        w2all = [b for p in w2b for b in p]
        for oc in range(8):
            bk = 3 + oc % 4
            yp = ps[:, bk, 0:T]
            mm_group(S, yp, [(w2s[:, f, oc * 128:(oc + 1) * 128], hm[:, f, :]) for f in range(NFC)], [hmb] + w2all, psb[bk])
            S.op("dve", lambda e, yp=yp, oc=oc, x=x: e.scalar_tensor_tensor(out=x[:, oc, :], in0=yp, scalar=0.5, in1=x[:, oc, :],
                                                                           op0=ALU.mult, op1=ALU.add),
                 reads=[psb[bk], xb], writes=[xb])
        S.dma("sp", xout[:, :, i * T:(i + 1) * T], x[:], xb, reads=[xb], writes=[outb])
    S.finish([outb] + xsb)
    S.run()
    C.close()


GELU_NATIVE = False
GC1 = 0.044715
GC2 = 1.5957691216057308


def phase_win(nc, x_d, w_d, g_d, qg_d, kg_d, bg_d, cos_d, sin_d, rot_d, q_o, k_o, v_o, xr_o, gy_o, gate_o, ntok):
    T = 512
    NT = ntok // T
    C = Ctx(nc)
    S = C.S
    const_setup(C)
    ws = C.sb([128, 8, INC], BF16, "ws")
    gs = C.sb([128, 8], F32, "gs"); gsb = S.buf("gs")
    qg = C.sb([128, 1], F32, "qg"); qgb = S.buf("qg")
    kg = C.sb([128, 1], F32, "kg"); kgb = S.buf("kg")
    bg = C.sb([128, 16], F32, "bg"); bgb = S.buf("bg")
    rot = C.sb([128, 128], BF16, "rot"); rotb = S.buf("rot")
    bones = C.sb([128, 128], BF16, "bones"); bonesb = S.buf("bones")
    S.dma("sp", gs[:], g_d, gsb, writes=[gsb])
    S.dma("sp", qg[:], qg_d, qgb, writes=[qgb])
    S.dma("sp", kg[:], kg_d, kgb, writes=[kgb])
    S.dma("sp", bg[:], bg_d, bgb, writes=[bgb])
    S.dma("pool", rot[:], rot_d, rotb, writes=[rotb])
    S.op("pool", lambda e: e.memset(bones[:], 0.0), writes=[bonesb])
    S.op("pool", lambda e: e.memset(bones[0:64, 0:64], 1.0), writes=[bonesb])
    S.op("pool", lambda e: e.memset(bones[64:128, 64:128], 1.0), writes=[bonesb])
    wb = load_weight(C, ws, w_d, 8, INC, "w", split=4)
    wall = [b for p in wb for b in p]

    x = C.sb([128, 8, T], F32, "x"); xb = S.buf("x")
    sq = C.sb([128, 8, T], BF16, "sq"); sqb = S.buf("sq")
    h = C.sb([128, 8, T], BF16, "h"); hb = S.buf("h")
    lnv = C.sb([128, T], F32, "lnv"); lnb = S.buf("lnv")
    rstd = C.sb([128, T], F32, "rstd"); rsb = S.buf("rstd")
    cs = C.sb([128, T], F32, "cs"); csb = S.buf("cs")
    sn = C.sb([128, T], F32, "sn"); snb = S.buf("sn")
    NR = 2
    sqc = [C.sb([128, T], BF16, "sqc%d" % i) for i in range(NR)]; sqcb = [S.buf("sqc%d" % i) for i in range(NR)]
    lnc = [C.sb([128, T], F32, "lnc%d" % i) for i in range(NR)]; lncb = [S.buf("lnc%d" % i) for i in range(NR)]
    rsc = [C.sb([128, T], F32, "rsc%d" % i) for i in range(NR)]; rscb = [S.buf("rsc%d" % i) for i in range(NR)]
    qn = [C.sb([128, T], BF16, "qn%d" % i) for i in range(NR)]; qnb = [S.buf("qn%d" % i) for i in range(NR)]
    t1 = [C.sb([128, T], F32, "t1%d" % i) for i in range(NR)]; t1b = [S.buf("t1%d" % i) for i in range(NR)]
    t2 = [C.sb([128, T], F32, "t2%d" % i) for i in range(NR)]; t2b = [S.buf("t2%d" % i) for i in range(NR)]
    qf = [C.sb([128, T], BF16, "qf%d" % i) for i in range(NR)]; qfb = [S.buf("qf%d" % i) for i in range(NR)]
    NSTG = 4
    stg = [C.sb([128, T], F32, "stg%d" % i) for i in range(NSTG)]; stgb = [S.buf("stg%d" % i) for i in range(NSTG)]
    tmp = [C.sb([128, T], F32, "tmp%d" % i) for i in range(2)]; tmpb = [S.buf("tmp%d" % i) for i in range(2)]
    vs = C.sb([128, 4, 256], BF16, "vs"); vsb = S.buf("vs")
    ps = C.psum
    psb = [S.buf("ps%d" % i) for i in range(8)]
    xin = feat(x_d)
    qo = feat(q_o); ko = feat(k_o); xro = feat(xr_o); gyo = feat(gy_o); gto = feat(gate_o)
    vo = v_o.rearrange("(n p) d -> p n d", p=128)
    outb = S.buf("out", par=True)
    stgi = [0]

    def proj(bank, col0):
        mm_group(S, ps[:, bank, :], [(ws[:, k, col0:col0 + 128], h[:, k, :]) for k in range(8)], [hb] + wall, psb[bank])

    for i in range(NT):
        tsl = slice(i * T, (i + 1) * T)
        S.dma("sp", x[:], xin[:, :, tsl], xb, writes=[xb])
        S.dma("sp", cs[:], cos_d[:, tsl], csb, writes=[csb])
        S.dma("sp", sn[:], sin_d[:, tsl], snb, writes=[snb])
        emit_rmsnorm(C, x, xb, h, hb, gs, gsb, T, C.ones, sq, sqb, psb[7], ps[:, 7, :], lnv, lnb, rstd, rsb)
        for c in range(10):
            r = c % NR
            bA, bB, bC = 3 * r, 3 * r + 1, 3 * r + 2
            gv, gvb = (qg, qgb) if c < 8 else (kg, kgb)
            proj(bA, c * 128)
            S.op("act", lambda e, r=r, bA=bA: e.activation(out=sqc[r][:], in_=ps[:, bA, :], func=AF.Square),
                 reads=[psb[bA]], writes=[sqcb[r]])
            mm_group(S, ps[:, bB, :], [(bones[:], sqc[r][:])], [sqcb[r], bonesb], psb[bB])
            S.op("act", lambda e, r=r, bB=bB: e.activation(out=lnc[r][:], in_=ps[:, bB, :], func=AF.Ln, scale=1.0 / HD, bias=C.eps[:]),
                 reads=[psb[bB], C.epsb], writes=[lncb[r]])
            S.op("act", lambda e, r=r: e.activation(out=rsc[r][:], in_=lnc[r][:], func=AF.Exp, scale=-0.5),
                 reads=[lncb[r]], writes=[rscb[r]])
            S.op("dve", lambda e, r=r, bA=bA, gv=gv: e.scalar_tensor_tensor(out=qn[r][:], in0=ps[:, bA, :], scalar=gv[:, 0:1], in1=rsc[r][:],
                                                                           op0=ALU.mult, op1=ALU.mult),
                 reads=[psb[bA], rscb[r], gvb], writes=[qnb[r]])
            mm_group(S, ps[:, bC, :], [(rot[:], qn[r][:])], [qnb[r], rotb], psb[bC])
            S.op("dve", lambda e, r=r: e.tensor_tensor(out=t1[r][:], in0=qn[r][:], in1=cs[:], op=ALU.mult),
                 reads=[qnb[r], csb], writes=[t1b[r]])
            S.op("dve", lambda e, r=r, bC=bC: e.tensor_tensor(out=t2[r][:], in0=ps[:, bC, :], in1=sn[:], op=ALU.mult),
                 reads=[psb[bC], snb], writes=[t2b[r]])
            S.op("dve", lambda e, r=r: e.tensor_tensor(out=qf[r][:], in0=t1[r][:], in1=t2[r][:], op=ALU.add),
                 reads=[t1b[r], t2b[r]], writes=[qfb[r]])
            dst = qo[:, c, tsl] if c < 8 else ko[:, c - 8, tsl]
            S.dma("pool", dst, qf[r][:], qfb[r], reads=[qfb[r]], writes=[outb])
        for tb in range(4):
            bank = 6 + tb % 2
            mm_group(S, ps[:, bank, 0:256], [(h[:, k, tb * 128:(tb + 1) * 128], ws[:, k, 1280:1536]) for k in range(8)],
                     [hb] + wall, psb[bank])
            S.op("act", lambda e, tb=tb, bank=bank: e.copy(out=vs[:, tb, :], in_=ps[:, bank, 0:256]), reads=[psb[bank]], writes=[vsb])
        S.dma("pool", vo[:, i * 4:(i + 1) * 4, :], vs[:], vsb, reads=[vsb], writes=[outb])
        for c in range(32):
            bank = 6 + c % 2
            si = stgi[0] % NSTG
            stgi[0] += 1
            st_, st_b = stg[si], stgb[si]
            proj(bank, 1536 + c * 128)
            pp = ps[:, bank, :]
            if c < 8:
                S.op("act", lambda e, st_=st_, pp=pp: e.copy(out=st_[:], in_=pp), reads=[psb[bank]], writes=[st_b])
                dst = xro[:, c, tsl]
            elif c < 16:
                if GELU_NATIVE:
                    S.op("act", lambda e, st_=st_, pp=pp: e.activation(out=st_[:], in_=pp, func=AF.Gelu_apprx_tanh),
                         reads=[psb[bank]], writes=[st_b])
                else:
                    ta, tab = tmp[0], tmpb[0]
                    tb_, tbb = tmp[1], tmpb[1]
                    S.op("act", lambda e, ta=ta, pp=pp: e.activation(out=ta[:], in_=pp, func=AF.Square), reads=[psb[bank]], writes=[tab])
                    S.op("dve", lambda e, ta=ta: e.tensor_scalar(out=ta[:], in0=ta[:], scalar1=GC1, scalar2=1.0, op0=ALU.mult, op1=ALU.add),
                         reads=[tab], writes=[tab])
                    S.op("dve", lambda e, ta=ta, tb_=tb_, pp=pp: e.tensor_tensor(out=tb_[:], in0=pp, in1=ta[:], op=ALU.mult),
                         reads=[tab, psb[bank]], writes=[tbb])
                    S.op("act", lambda e, tb_=tb_: e.activation(out=tb_[:], in_=tb_[:], func=AF.Sigmoid, scale=GC2), reads=[tbb], writes=[tbb])
                    S.op("dve", lambda e, st_=st_, tb_=tb_, pp=pp: e.tensor_tensor(out=st_[:], in0=pp, in1=tb_[:], op=ALU.mult),
                         reads=[tbb, psb[bank]], writes=[st_b])
                dst = gyo[:, c - 8, tsl]
            else:
                S.op("act", lambda e, st_=st_, pp=pp, c=c: e.activation(out=st_[:], in_=pp, func=AF.Sigmoid, bias=bg[:, c - 16:c - 15]),
                     reads=[psb[bank], bgb], writes=[st_b])
                dst = gto[:, c - 16, tsl]
            S.dma("pool", dst, st_[:], st_b, reads=[st_b], writes=[outb])
    S.finish([outb, xb, csb, snb])
    S.run()
    C.close()


def phase_att(nc, q_d, k_d, v_d, o_d, nq, nk):
    T = 512
    NQT = nq // T
    NKC = nk // 128
    NI = NKC // 2
    C = Ctx(nc)
    S = C.S
    kk = C.sb([128, 2, nk], BF16, "kk"); kkb = [S.buf("kk%d" % i) for i in range(4)]
    vv = C.sb([128, NKC, 4, 128], BF16, "vv")
    VSPL = max(1, NKC // 8)
    vvb = [S.buf("vv%d" % i) for i in range(NKC // VSPL)]
    for g in range(4):
        pb = (g // 2) * 64
        S.dma("sp", kk[pb:pb + 64, g % 2, :], k_d[g * 64:(g + 1) * 64, :], kkb[g], writes=[kkb[g]])
    vin = v_d.rearrange("(n p) d -> p n d", p=128)
    for s in range(NKC // VSPL):
        ks = slice(s * VSPL, (s + 1) * VSPL)
        S.op("pool", lambda e, ks=ks: e.memset(vv[:, ks, :, 64:128], 1.0), writes=[vvb[s]])
        for g in range(4):
            S.dma("sp", vv[:, ks, g, 0:64], vin[:, ks, g * 64:(g + 1) * 64], vvb[s], writes=[vvb[s]])
    qq = [C.sb([128, 8, T], BF16, "qq%d" % i) for i in range(2)]
    qqb = [S.buf("qq%d" % i) for i in range(2)]
    NP = 3
    P = [C.sb([128, 2, T], BF16, "P%d" % i) for i in range(NP)]
    Pb = [S.buf("P%d" % i) for i in range(NP)]
    rd = C.sb([64, T], F32, "rd"); rdb = S.buf("rd")
    ot = C.sb([64, 16, T], BF16, "ot"); otb = S.buf("ot")
    ps = C.psum
    psb = [S.buf("ps%d" % i) for i in range(8)]
    qin0 = q_d[0:512, :].rearrange("(j p) t -> p j t", p=64)
    qin1 = q_d[512:1024, :].rearrange("(j p) t -> p j t", p=64)
    oo = o_d.rearrange("(h p) t -> p h t", p=64)
    outb = S.buf("out", par=True)

    def loadq(i):
        tsl = slice(i * T, (i + 1) * T)
        S.dma("sp", qq[i % 2][0:64, :, :], qin0[:, :, tsl], qqb[i % 2], writes=[qqb[i % 2]])
        S.dma("sp", qq[i % 2][64:128, :, :], qin1[:, :, tsl], qqb[i % 2], writes=[qqb[i % 2]])

    its = [(i, hp, kc) for i in range(NQT) for hp in range(8) for kc in range(NKC)]

    def emit_qk(idx):
        i, hp, kc = its[idx]
        pr = idx % 2
        for s in range(2):
            pb = 64 * s
            g = (hp + 8 * s) // 4
            S.op("pe", lambda e, pr=pr, s=s, pb=pb, g=g, kc=kc, i=i, hp=hp: e.matmul(
                ps[:, 2 * pr + s, :], lhsT=kk[pb:pb + 64, g % 2, kc * 128:(kc + 1) * 128], rhs=qq[i % 2][pb:pb + 64, hp, :],
                start=True, stop=True),
                reads=[kkb[g], qqb[i % 2]], writes=[psb[2 * pr + s]], inc=(s == 1))

    loadq(0)
    emit_qk(0)
    for idx, (i, hp, kc) in enumerate(its):
        if hp == 0 and kc == 0 and i + 1 < NQT:
            loadq(i + 1)
        if idx + 1 < len(its):
            emit_qk(idx + 1)
        pr = idx % 2
        pi = idx % NP
        last = (kc == NKC - 1)
        S.op("act", lambda e, pr=pr, pi=pi: e.activation(out=P[pi][:], in_=ps[:, 2 * pr:2 * pr + 2, :], func=AF.Exp, scale=HD ** -0.5),
             reads=[psb[2 * pr], psb[2 * pr + 1]], writes=[Pb[pi]])
        for s in range(2):
            g = (hp + 8 * s) // 4
            ob = 4 + 2 * (hp % 2) + s
            S.op("pe", lambda e, ob=ob, kc=kc, g=g, pi=pi, s=s, last=last: e.matmul(
                ps[:, ob, :], lhsT=vv[:, kc, g, :], rhs=P[pi][:, s, :], start=(kc == 0), stop=last),
                reads=[Pb[pi], vvb[kc // VSPL]], writes=[psb[ob]], inc=(last and s == 1))
        if last:
            for s in range(2):
                h = hp + 8 * s
                ob = 4 + 2 * (hp % 2) + s
                S.op("dve", lambda e, ob=ob: e.reciprocal(out=rd[:], in_=ps[64:128, ob, :]), reads=[psb[ob]], writes=[rdb])
                S.op("dve", lambda e, ob=ob, h=h: e.tensor_tensor(out=ot[:, h, :], in0=ps[0:64, ob, :], in1=rd[:], op=ALU.mult),
                     reads=[psb[ob], rdb], writes=[otb])
            if hp == 7:
                S.dma("pool", oo[:, :, i * T:(i + 1) * T], ot[:], otb, reads=[otb], writes=[outb])
    S.finish([outb] + qqb)
    S.run()
    C.close()


A_CLAMP = 0.99999994


def phase_lru(nc, xr_d, gy_d, cw_d, cb_d, wa_d, wx_d, ba_d, bx_d, lam_d, m_o, xcs_d, hfs_d, ntok, NCH=4):
    T = 512
    NT = ntok // T
    C = Ctx(nc)
    S = C.S
    cst = C.sb([128, 2], F32, "cst"); cstb = S.buf("cst")
    S.op("pool", lambda e: e.memset(cst[:, 0:1], 1.0), writes=[cstb])
    cw = C.sb([128, NCH, 4], F32, "cw"); cwb = S.buf("cw")
    cb = C.sb([128, NCH], F32, "cb"); cbb = S.buf("cb")
    ba = C.sb([128, 2, NCH], F32, "ba"); bab = S.buf("ba")
    bx = C.sb([128, 2, NCH], F32, "bx"); bxb = S.buf("bx")
    lam = C.sb([128, 2, NCH], F32, "lam"); lamb = S.buf("lam")
    sc = C.sb([128, 2, NCH], F32, "sc"); scb = S.buf("sc")
    was = C.sb([128, 2, NCH, 128], BF16, "was"); wab = S.buf("was")
    wxs = C.sb([128, 2, NCH, 128], BF16, "wxs"); wxb = S.buf("wxs")
    S.dma("sp", cw[:], cw_d, cwb, writes=[cwb])
    S.dma("sp", cb[:], cb_d, cbb, writes=[cbb])
    S.dma("sp", ba[:], ba_d, bab, writes=[bab])
    S.dma("sp", bx[:], bx_d, bxb, writes=[bxb])
    S.dma("sp", lam[:], lam_d, lamb, writes=[lamb])
    S.dma("pool", was[:], wa_d.rearrange("d c p m -> p d c m"), wab, writes=[wab])
    S.dma("pool", wxs[:], wx_d.rearrange("d c p m -> p d c m"), wxb, writes=[wxb])
    S.op("act", lambda e: e.activation(out=sc[:], in_=lam[:], func=AF.Exp, scale=-1.0), reads=[lamb], writes=[scb])
    S.op("act", lambda e: e.activation(out=sc[:], in_=sc[:], func=AF.Ln, bias=cst[:, 0:1]), reads=[scb, cstb], writes=[scb])
    S.op("dve", lambda e: e.tensor_scalar(out=sc[:], in0=sc[:], scalar1=-8.0, scalar2=None, op0=ALU.mult), reads=[scb], writes=[scb])

    NR = 2
    def mk(name, dt=F32, w=T):
        return ([C.sb([128, w], dt, "%s%d" % (name, i)) for i in range(NR)], [S.buf("%s%d" % (name, i)) for i in range(NR)])
    xw = C.sb([128, NCH, T + 3], F32, "xw"); xwb = S.buf("xw")
    xc, xcb_ = mk("xc"); xb16, xb16b = mk("xb16", BF16)
    rr, rrb = mk("rr"); ii, iib = mk("ii"); aa, aab = mk("aa"); a2, a2b = mk("a2"); uu, uub = mk("uu")
    hh, hhb = mk("hh"); hf, hfb = mk("hf"); gy, gyb = mk("gy"); mo, mob = mk("mo", BF16)
    car = C.sb([128, NCH], F32, "car"); carb = [S.buf("car%d" % c) for c in range(NCH)]
    ps = C.psum
    psb = [S.buf("ps%d" % i) for i in range(8)]
    xrv = feat(xr_d); gyv = feat(gy_d); mv = feat(m_o); xcv = feat(xcs_d); hfv = feat(hfs_d)
    xcsb = S.buf("xcs", par=True); hfsb = S.buf("hfs", par=True); outb = S.buf("out", par=True)
    cnt = [0]

    def gates(d, c, r, reverse, first):
        bk = 2 * (cnt[0] % 4)
        cnt[0] += 1
        S.op("act", lambda e: e.copy(out=xb16[r][:], in_=xc[r][:]), reads=[xcb_[r]], writes=[xb16b[r]])
        mm_group(S, ps[:, bk, :], [(was[:, d, c, :], xb16[r][:])], [xb16b[r], wab], psb[bk])
        mm_group(S, ps[:, bk + 1, :], [(wxs[:, d, c, :], xb16[r][:])], [xb16b[r], wxb], psb[bk + 1])
        S.op("act", lambda e: e.activation(out=rr[r][:], in_=ps[:, bk, :], func=AF.Sigmoid, bias=ba[:, d, c:c + 1]),
             reads=[psb[bk], bab], writes=[rrb[r]])
        S.op("act", lambda e: e.activation(out=ii[r][:], in_=ps[:, bk + 1, :], func=AF.Sigmoid, bias=bx[:, d, c:c + 1]),
             reads=[psb[bk + 1], bxb], writes=[iib[r]])
        S.op("act", lambda e: e.activation(out=aa[r][:], in_=rr[r][:], func=AF.Exp, scale=sc[:, d, c:c + 1]),
             reads=[rrb[r], scb], writes=[aab[r]])
        S.op("dve", lambda e: e.scalar_tensor_tensor(out=a2[r][:], in0=aa[r][:], scalar=A_CLAMP, in1=aa[r][:], op0=ALU.min, op1=ALU.mult),
             reads=[aab[r]], writes=[a2b[r]])
        S.op("act", lambda e: e.activation(out=a2[r][:], in_=a2[r][:], func=AF.Ln, scale=-1.0, bias=cst[:, 0:1]),
             reads=[a2b[r], cstb], writes=[a2b[r]])
        S.op("act", lambda e: e.activation(out=a2[r][:], in_=a2[r][:], func=AF.Exp, scale=0.5), reads=[a2b[r]], writes=[a2b[r]])
        S.op("dve", lambda e: e.tensor_tensor(out=uu[r][:], in0=xc[r][:], in1=ii[r][:], op=ALU.mult),
             reads=[xcb_[r], iib[r]], writes=[uub[r]])
        S.op("dve", lambda e: e.tensor_tensor(out=uu[r][:], in0=uu[r][:], in1=a2[r][:], op=ALU.mult),
             reads=[uub[r], a2b[r]], writes=[uub[r]])
        init = 0.0 if first else car[:, c:c + 1]
        rd_ = [aab[r], uub[r]] + ([] if first else [carb[c]])
        if not reverse:
            S.op("dve", lambda e: e.tensor_tensor_scan(out=hh[r][:], data0=aa[r][:], data1=uu[r][:], initial=init, op0=ALU.mult, op1=ALU.add),
                 reads=rd_, writes=[hhb[r]])
            S.op("act", lambda e: e.copy(out=car[:, c:c + 1], in_=hh[r][:, T - 1:T]), reads=[hhb[r]], writes=[carb[c]])
        else:
            S.op("dve", lambda e: e.tensor_tensor_scan(out=hh[r][:, ::-1], data0=aa[r][:, ::-1], data1=uu[r][:, ::-1], initial=init,
                                                        op0=ALU.mult, op1=ALU.add),
                 reads=rd_, writes=[hhb[r]])
            S.op("act", lambda e: e.copy(out=car[:, c:c + 1], in_=hh[r][:, 0:1]), reads=[hhb[r]], writes=[carb[c]])

    k = 0
    for i in range(NT):
        t0 = i * T
        lo = max(t0 - 2, 0); hi = min(t0 + T + 1, ntok)
        if i == 0:
            S.op("pool", lambda e: e.memset(xw[:, :, 0:2], 0.0), writes=[xwb])
        if i == NT - 1:
            S.op("pool", lambda e: e.memset(xw[:, :, T + 2:T + 3], 0.0), writes=[xwb])
        S.dma("sp", xw[:, :, lo - (t0 - 2):hi - (t0 - 2)], xrv[:, :, lo:hi], xwb, writes=[xwb])
        for c in range(NCH):
            r = k % NR; k += 1
            S.op("dve", lambda e, c=c, r=r: e.tensor_scalar(out=xc[r][:], in0=xw[:, c, 0:T], scalar1=cw[:, c, 0:1], scalar2=cb[:, c:c + 1],
                                                             op0=ALU.mult, op1=ALU.add),
                 reads=[xwb, cwb, cbb], writes=[xcb_[r]])
            for j in range(1, 4):
                S.op("dve", lambda e, c=c, r=r, j=j: e.scalar_tensor_tensor(out=xc[r][:], in0=xw[:, c, j:j + T], scalar=cw[:, c, j:j + 1],
                                                                              in1=xc[r][:], op0=ALU.mult, op1=ALU.add),
                     reads=[xwb, cwb, xcb_[r]], writes=[xcb_[r]])
            S.dma("pool", xcv[:, c, t0:t0 + T], xc[r][:], xcb_[r], reads=[xcb_[r]], writes=[xcsb])
            gates(0, c, r, False, i == 0)
            S.dma("pool", hfv[:, c, t0:t0 + T], hh[r][:], hhb[r], reads=[hhb[r]], writes=[hfsb])
    for i in range(NT - 1, -1, -1):
        t0 = i * T
        for c in range(NCH):
            r = k % NR; k += 1
            S.dma("sp", xc[r][:], xcv[:, c, t0:t0 + T], xcb_[r], reads=[xcsb], writes=[xcb_[r]])
            S.dma("sp", hf[r][:], hfv[:, c, t0:t0 + T], hfb[r], reads=[hfsb], writes=[hfb[r]])
            S.dma("sp", gy[r][:], gyv[:, c, t0:t0 + T], gyb[r], writes=[gyb[r]])
            gates(1, c, r, True, i == NT - 1)
            S.op("dve", lambda e, r=r: e.tensor_tensor(out=hf[r][:], in0=hf[r][:], in1=hh[r][:], op=ALU.add),
                 reads=[hfb[r], hhb[r]], writes=[hfb[r]])
            S.op("dve", lambda e, r=r: e.tensor_tensor(out=mo[r][:], in0=hf[r][:], in1=gy[r][:], op=ALU.mult),
                 reads=[hfb[r], gyb[r]], writes=[mob[r]])
            S.dma("pool", mv[:, c, t0:t0 + T], mo[r][:], mob[r], reads=[mob[r]], writes=[outb])
    S.finish([outb, xcsb, hfsb])
    S.run()
    C.close()


def phase_mix(nc, o_d, m_d, gate_d, x1_d, wao_d, wro_d, wout_d, out_d, ntok):
    T = 512
    NT = ntok // T
    C = Ctx(nc)
    S = C.S
    wao = C.sb([128, 8, D], BF16, "wao"); wro = C.sb([128, 8, D], BF16, "wro"); wout = C.sb([128, 8, D], BF16, "wout")
    waob = [b for p in load_weight(C, wao, wao_d, 8, D, "wao", split=1) for b in p]
    wrob = [b for p in load_weight(C, wro, wro_d, 8, D, "wro", split=1) for b in p]
    woutb = [b for p in load_weight(C, wout, wout_d, 8, D, "wout", split=1) for b in p]
    o = C.sb([128, 8, T], BF16, "o"); ob = S.buf("o")
    m = C.sb([128, 8, T], BF16, "m"); mb = S.buf("m")
    gt = C.sb([128, 16, T], F32, "gt"); gtb = S.buf("gt")
    x1 = [C.sb([128, 8, T], F32, "x1%d" % i) for i in range(2)]; x1b = [S.buf("x1%d" % i) for i in range(2)]
    tA = [C.sb([128, T], F32, "tA%d" % i) for i in range(2)]; tAb = [S.buf("tA%d" % i) for i in range(2)]
    tB = [C.sb([128, T], F32, "tB%d" % i) for i in range(2)]; tBb = [S.buf("tB%d" % i) for i in range(2)]
    mg = C.sb([128, 8, T], BF16, "mg"); mgb = S.buf("mg")
    ps = C.psum
    psb = [S.buf("ps%d" % i) for i in range(8)]
    ov = feat(o_d); mv = feat(m_d); gv = feat(gate_d); xv = feat(x1_d); outv = feat(out_d)
    outb = S.buf("out", par=True)
    for i in range(NT):
        tsl = slice(i * T, (i + 1) * T)
        x = x1[i % 2]; xb = x1b[i % 2]
        S.dma("sp", o[:], ov[:, :, tsl], ob, writes=[ob])
        S.dma("sp", m[:], mv[:, :, tsl], mb, writes=[mb])
        S.dma("sp", gt[:], gv[:, :, tsl], gtb, writes=[gtb])
        S.dma("sp", x[:], xv[:, :, tsl], xb, writes=[xb])
        for oc in range(8):
            r = oc % 2
            bA, bB = 2 * r, 2 * r + 1
            osl = slice(oc * 128, (oc + 1) * 128)
            mm_group(S, ps[:, bA, :], [(wao[:, c, osl], o[:, c, :]) for c in range(8)], [ob] + waob, psb[bA])
            mm_group(S, ps[:, bB, :], [(wro[:, c, osl], m[:, c, :]) for c in range(8)], [mb] + wrob, psb[bB])
            S.op("dve", lambda e, r=r, bA=bA, oc=oc: e.tensor_tensor(out=tA[r][:], in0=ps[:, bA, :], in1=gt[:, oc, :], op=ALU.mult),
                 reads=[psb[bA], gtb], writes=[tAb[r]])
            S.op("dve", lambda e, r=r, bB=bB, oc=oc: e.tensor_tensor(out=tB[r][:], in0=ps[:, bB, :], in1=gt[:, 8 + oc, :], op=ALU.mult),
                 reads=[psb[bB], gtb], writes=[tBb[r]])
            S.op("dve", lambda e, r=r, oc=oc: e.tensor_tensor(out=mg[:, oc, :], in0=tA[r][:], in1=tB[r][:], op=ALU.add),
                 reads=[tAb[r], tBb[r]], writes=[mgb])
        for oc in range(8):
            bk = 4 + oc % 4
            osl = slice(oc * 128, (oc + 1) * 128)
            mm_group(S, ps[:, bk, :], [(wout[:, c, osl], mg[:, c, :]) for c in range(8)], [mgb] + woutb, psb[bk])
            S.op("dve", lambda e, bk=bk, oc=oc, x=x: e.tensor_tensor(out=x[:, oc, :], in0=ps[:, bk, :], in1=x[:, oc, :], op=ALU.add),
                 reads=[psb[bk], xb], writes=[xb])
        S.dma("pool", outv[:, :, tsl], x[:], xb, reads=[xb], writes=[outb])
    S.finish([outb, ob, mb, gtb] + x1b)
    S.run()
    C.close()


NTOK = SEQ // 2
BF = ml_dtypes.bfloat16


def _din(nc, n, s, dt=F32):
    return nc.dram_tensor(n, list(s), dt, kind="ExternalInput").ap()


def _dout(nc, n, s, dt=F32):
    return nc.dram_tensor(n, list(s), dt, kind="ExternalOutput").ap()


def _dint(nc, n, s, dt=F32):
    return nc.dram_tensor(n, list(s), dt, kind="Internal").ap()


def build_A(ntok=NTOK):
    nc = bass.Bass("TRN2", target_bir_lowering=False)
    xT = _din(nc, "xT", [D, ntok]); w1 = _din(nc, "w1", [D, 2 * DFF]); w2 = _din(nc, "w2", [DFF, D]); g1 = _din(nc, "g1", [128, 8])
    win = _din(nc, "win", [D, INC]); g2 = _din(nc, "g2", [128, 8]); qg = _din(nc, "qg", [128, 1]); kg = _din(nc, "kg", [128, 1])
    bg = _din(nc, "bg", [128, 16]); cos = _din(nc, "cos", [128, ntok]); sin = _din(nc, "sin", [128, ntok]); rot = _din(nc, "rot", [128, 128])
    x1T = _dout(nc, "x1T", [D, ntok]); q = _dout(nc, "q", [D, ntok], BF16); k = _dout(nc, "k", [256, ntok], BF16)
    v = _dout(nc, "v", [ntok, 256], BF16); xr = _dout(nc, "xr", [D, ntok]); gy = _dout(nc, "gy", [D, ntok]); gate = _dout(nc, "gate", [2 * D, ntok])
    phase_ffn(nc, xT, x1T, w1, w2, g1, ntok)
    phase_win(nc, x1T, win, g2, qg, kg, bg, cos, sin, rot, q, k, v, xr, gy, gate, ntok)
    return nc


def build_BL(nq=NTOK, nk=SEQ):
    nc = bass.Bass("TRN2", target_bir_lowering=False)
    q = _din(nc, "q", [D, nq], BF16); k = _din(nc, "k", [256, nk], BF16); v = _din(nc, "v", [nk, 256], BF16)
    xr = _din(nc, "xr", [512, nk]); gy = _din(nc, "gy", [512, nk]); cw = _din(nc, "cw", [128, 4, 4]); cb = _din(nc, "cb", [128, 4])
    wa = _din(nc, "wa", [2, 4, 128, 128]); wx = _din(nc, "wx", [2, 4, 128, 128])
    ba = _din(nc, "ba", [128, 2, 4]); bx = _din(nc, "bx", [128, 2, 4]); lam = _din(nc, "lam", [128, 2, 4])
    o = _dout(nc, "o", [D, nq], BF16); m = _dout(nc, "m", [512, nk], BF16)
    xcs = _dint(nc, "xcs", [512, nk]); hfs = _dint(nc, "hfs", [512, nk])
    phase_lru(nc, xr, gy, cw, cb, wa, wx, ba, bx, lam, m, xcs, hfs, nk, 4)
    phase_att(nc, q, k, v, o, nq, nk)
    return nc


def build_C(ntok=NTOK):
    nc = bass.Bass("TRN2", target_bir_lowering=False)
    o = _din(nc, "o", [D, ntok], BF16); m = _din(nc, "m", [D, ntok], BF16); gate = _din(nc, "gate", [2 * D, ntok]); x1T = _din(nc, "x1T", [D, ntok])
    wao = _din(nc, "wao", [D, D]); wro = _din(nc, "wro", [D, D]); wout = _din(nc, "wout", [D, D])
    w1 = _din(nc, "w1", [D, 2 * DFF]); w2 = _din(nc, "w2", [DFF, D]); g3 = _din(nc, "g3", [128, 8])
    x2T = _dint(nc, "x2T", [D, ntok]); x3T = _dout(nc, "x3T", [D, ntok])
    phase_mix(nc, o, m, gate, x1T, wao, wro, wout, x2T, ntok)
    phase_ffn(nc, x2T, x3T, w1, w2, g3, ntok)
    return nc


def rope_tables(pos):
    inv = (ROPE_THETA ** (-np.arange(0, 32, 2, dtype=np.float32) / np.float32(32))).astype(np.float32)
    row = (pos // GRID_W).astype(np.float32)
    col = (pos % GRID_W).astype(np.float32)
    cc = np.zeros((128, len(pos)), np.float32)
    ss = np.zeros((128, len(pos)), np.float32)
    for p in range(128):
        f = p % 64
        ang = ((row if f < 32 else col) * inv[f % 16]).astype(np.float32)
        cc[p] = np.cos(ang)
        ss[p] = np.sin(ang) * (-1.0 if (f % 32) < 16 else 1.0)
    return cc, ss


def rot_matrix():
    r = np.zeros((128, 128), np.float32)
    for m_ in range(128):
        f = m_ % 64
        pf = f + 16 if (f % 32) < 16 else f - 16
        r[m_ - f + pf, m_] = 1.0
    return r


def _pc(vec):
    return np.ascontiguousarray(vec.reshape(-1, 128).T)


def _bd(w):
    o = np.zeros((2, 4, 128, 128), np.float32)
    for d in range(2):
        for c in range(4):
            o[d, c, :64, :64] = w[d, 2 * c]
            o[d, c, 64:, 64:] = w[d, 2 * c + 1]
    return o


def _run(nc, in_maps):
    res = _bu.run_bass_kernel_spmd(nc, in_maps, core_ids=list(range(len(in_maps))))
    return res.results


LAYER_W = ["w1a", "w2a", "g1", "win", "g2", "qg", "kg", "bg", "wao", "wro", "wout", "w1b", "w2b", "g3",
           "cw", "cb", "wa", "wx", "ba", "bx", "lam"]


def build_fused(seq=SEQ, depth=2):
    nc = bass.Bass("TRN2", target_bir_lowering=False)
    shp = {"w1a": [D, 2 * DFF], "w2a": [DFF, D], "g1": [128, 8], "win": [D, INC], "g2": [128, 8], "qg": [128, 1], "kg": [128, 1],
           "bg": [128, 16], "wao": [D, D], "wro": [D, D], "wout": [D, D], "w1b": [D, 2 * DFF], "w2b": [DFF, D], "g3": [128, 8],
           "cw": [128, 8, 4], "cb": [128, 8], "wa": [2, 8, 128, 128], "wx": [2, 8, 128, 128], "ba": [128, 2, 8], "bx": [128, 2, 8],
           "lam": [128, 2, 8]}
    xT = _din(nc, "xT", [D, seq])
    cos = _din(nc, "cos", [128, seq]); sin = _din(nc, "sin", [128, seq]); rot = _din(nc, "rot", [128, 128])
    W = [{n: _din(nc, "%s_%d" % (n, l), shp[n]) for n in LAYER_W} for l in range(depth)]
    outT = _dout(nc, "outT", [D, seq])
    x1 = _dint(nc, "s_x1", [D, seq]); x2 = _dint(nc, "s_x2", [D, seq]); x3 = _dint(nc, "s_x3", [D, seq])
    q = _dint(nc, "s_q", [D, seq], BF16); k = _dint(nc, "s_k", [256, seq], BF16); v = _dint(nc, "s_v", [seq, 256], BF16)
    xr = _dint(nc, "s_xr", [D, seq]); gy = _dint(nc, "s_gy", [D, seq]); gate = _dint(nc, "s_gate", [2 * D, seq])
    o = _dint(nc, "s_o", [D, seq], BF16); m = _dint(nc, "s_m", [D, seq], BF16)
    xcs = _dint(nc, "s_xcs", [D, seq]); hfs = _dint(nc, "s_hfs", [D, seq])
    xin = xT
    for l in range(depth):
        w = W[l]
        xo = outT if l == depth - 1 else x3
        phase_ffn(nc, xin, x1, w["w1a"], w["w2a"], w["g1"], seq)
        phase_win(nc, x1, w["win"], w["g2"], w["qg"], w["kg"], w["bg"], cos, sin, rot, q, k, v, xr, gy, gate, seq)
        phase_lru(nc, xr, gy, w["cw"], w["cb"], w["wa"], w["wx"], w["ba"], w["bx"], w["lam"], m, xcs, hfs, seq, 8)
        phase_att(nc, q, k, v, o, seq, seq)
        phase_mix(nc, o, m, gate, x1, w["wao"], w["wro"], w["wout"], x2, seq)
        phase_ffn(nc, x2, xo, w["w1b"], w["w2b"], w["g3"], seq)
        xin = x3
    return nc


def _bd8(w):
    o = np.zeros((2, 8, 128, 128), np.float32)
    for d in range(2):
        for c in range(8):
            o[d, c, :64, :64] = w[d, 2 * c]
            o[d, c, 64:, 64:] = w[d, 2 * c + 1]
    return o


def layer_inputs(l, p):
    f = lambda a: np.ascontiguousarray(np.asarray(a, dtype=np.float32))
    t3 = lambda a: np.ascontiguousarray(f(a).reshape(2, 8, 128).transpose(2, 0, 1))
    d = {"w1a": f(p["ffn1_w1"][l]), "w2a": f(p["ffn1_w2"][l]), "g1": _pc(f(p["ffn1_norm"][l])), "win": f(p["w_in"][l]),
         "g2": _pc(f(p["mix_norm"][l])), "qg": np.tile(f(p["q_norm"][l]), 2)[:, None].copy(), "kg": np.tile(f(p["k_norm"][l]), 2)[:, None].copy(),
         "bg": _pc(f(p["b_gate"][l]).reshape(-1)), "wao": f(p["w_attn_o"][l]), "wro": f(p["w_rnn_o"][l]), "wout": f(p["w_out"][l]),
         "w1b": f(p["ffn2_w1"][l]), "w2b": f(p["ffn2_w2"][l]), "g3": _pc(f(p["ffn2_norm"][l])),
         "cw": np.ascontiguousarray(f(p["conv_w"][l]).reshape(4, 8, 128).transpose(2, 1, 0)), "cb": _pc(f(p["conv_b"][l])),
         "wa": _bd8(f(p["lru_wa"][l])), "wx": _bd8(f(p["lru_wx"][l])), "ba": t3(p["lru_ba"][l]), "bx": t3(p["lru_bx"][l]),
         "lam": t3(p["lru_lambda"][l])}
    return {"%s_%d" % (n, l): a for n, a in d.items()}


def kernel(**p):
    x = np.ascontiguousarray(np.asarray(p["x"], dtype=np.float32))
    nb, seq, _ = x.shape
    depth = np.asarray(p["ffn1_norm"]).shape[0]
    nc = build_fused(seq, depth)
    cc, ss = rope_tables(np.arange(seq))
    common = {"cos": cc, "sin": ss, "rot": rot_matrix()}
    for l in range(depth):
        common.update(layer_inputs(l, p))
    in_maps = [dict(common, xT=np.ascontiguousarray(x[b].T)) for b in range(nb)]
    res = _run(nc, in_maps)
    out = np.empty_like(x)
    for b in range(nb):
        out[b] = res[b]["outT"].T
    return out
```
